# Optimizing a Trainium2 kernel written in Bass

```python
import math
import jax
import jax.numpy as jnp
from jax import lax
import numpy as np

D_MODEL = 2048
BATCH = 4
SEQ = 8192
DEPTH = 4
DEC_BATCH = 16
DEC_SEQ = 64
PAST_LEN = 2048

CHUNK = 64
N_META = 16
N_MIXERS = 3
ROPE_THETA = 500000.0
NORM_EPS = 1e-6
Q_BLOCK = 128
NEG_INF = -1e30

DA_HEADS = 8
DA_HD = 128
DA_QK = DA_HEADS * 2 * DA_HD
DA_WIDTH = DA_HEADS * 2 * DA_HD
SW_HEADS = 32
SW_KV = 4
SW_HD = 64
SW_WIDTH = SW_HEADS * SW_HD
WINDOW = 128
DS_HEADS = 16
DS_KV = 4
DS_HD = 128
DS_WIDTH = DS_HEADS * DS_HD
IDX_HEADS = 16
IDX_HD = 64
TOPK_MAX = 256

kernel_name = 'hybrid_streaming_encoder_step'


def rms_norm(x, g):
    xf = x.astype(jnp.float32)
    y = xf * lax.rsqrt(jnp.mean(xf * xf, axis=-1, keepdims=True) + NORM_EPS)
    return (y * g.astype(jnp.float32)).astype(x.dtype)


def partial_rope(x, pos):
    d = x.shape[-1]
    rd = d // 4
    half = rd // 2
    inv_freq = ROPE_THETA ** (-(jnp.arange(half, dtype=jnp.float32) * 2.0 / rd))
    ang = pos.astype(jnp.float32)[:, None] * inv_freq[None, :]
    cos = jnp.cos(ang)[None, :, None, :]
    sin = jnp.sin(ang)[None, :, None, :]
    xf = x.astype(jnp.float32)
    x1 = xf[..., :half]
    x2 = xf[..., half:rd]
    out = jnp.concatenate([x1 * cos - x2 * sin, x2 * cos + x1 * sin, xf[..., rd:]], axis=-1)
    return out.astype(x.dtype)


def split_cols(p, sizes):
    bounds = [int(b) for b in np.cumsum(sizes)[:-1]]
    return jnp.split(p, bounds, axis=-1)


def prompt_chunk_ids(n):
    return (jnp.arange(n) - N_META) // CHUNK


def to_blocks(a, tq):
    a = jnp.pad(a, [(0, 0), (0, tq - a.shape[1])] + [(0, 0)] * (a.ndim - 2))
    a = a.reshape(a.shape[0], tq // Q_BLOCK, Q_BLOCK, *a.shape[2:])
    return jnp.moveaxis(a, 1, 0)


def from_blocks(o, t):
    o = jnp.moveaxis(o, 0, 1)
    return o.reshape(o.shape[0], o.shape[1] * o.shape[2], *o.shape[3:])[:, :t]


def diff_combine(q, k, v, lam, mask):
    s = jnp.einsum('bqhjd,bkhjd->bhjqk', q, k).astype(jnp.float32) * (DA_HD ** -0.5)
    if mask is not None:
        s = jnp.where(mask, s, NEG_INF)
    p = jax.nn.softmax(s, axis=-1)
    w = p[:, :, 0] - lam * p[:, :, 1]
    return jnp.einsum('bhqk,bkhe->bqhe', w.astype(v.dtype), v)


def diff_mixer(h, pos, w_in, w_out, lam_q1, lam_k1, lam_q2, lam_k2, subln, lam_init, cache_k=None, cache_v=None):
    b, t, _ = h.shape
    q, k, v, z = split_cols(h @ w_in, (DA_QK, DA_QK, DA_WIDTH, DA_WIDTH))
    q = partial_rope(q.reshape(b, t, 2 * DA_HEADS, DA_HD), pos).reshape(b, t, DA_HEADS, 2, DA_HD)
    k = partial_rope(k.reshape(b, t, 2 * DA_HEADS, DA_HD), pos).reshape(b, t, DA_HEADS, 2, DA_HD)
    v = v.reshape(b, t, DA_HEADS, 2 * DA_HD)
    f32 = jnp.float32
    lam = (jnp.exp(jnp.sum(lam_q1.astype(f32) * lam_k1.astype(f32)))
           - jnp.exp(jnp.sum(lam_q2.astype(f32) * lam_k2.astype(f32))) + lam_init)
    if cache_k is None:
        tq = -(-t // Q_BLOCK) * Q_BLOCK
        k_chunk = prompt_chunk_ids(t)
        q_chunk = prompt_chunk_ids(tq).reshape(-1, Q_BLOCK)

        def block(args):
            qb, qc = args
            return diff_combine(qb, k, v, lam, k_chunk[None, :] <= qc[:, None])

        o = from_blocks(lax.map(block, (to_blocks(q, tq), q_chunk)), t)
    else:
        nb, npast = cache_k.shape[:2]
        k_all = jnp.concatenate([cache_k.reshape(nb, npast, DA_HEADS, 2, DA_HD), k], axis=1)
        v_all = jnp.concatenate([cache_v, v], axis=1)
        o = diff_combine(q, k_all, v_all, lam, None)
    o = rms_norm(o, subln) * (1.0 - lam_init)
    o = o.reshape(b, t, DA_WIDTH) * jax.nn.silu(z)
    return o @ w_out, k.reshape(b, t, DA_HEADS, 2 * DA_HD), v


def sink_attend(s, v, sinks, eq):
    sk = sinks.astype(jnp.float32).reshape(SW_KV, -1)[:, :, None]
    m = jnp.maximum(jnp.max(s, axis=-1), sk)
    e = jnp.exp(s - m[..., None])
    p = e / (jnp.sum(e, axis=-1) + jnp.exp(sk - m))[..., None]
    return jnp.einsum(eq, p.astype(v.dtype), v)


def swa_mixer(h, pos, w_in, w_out, sinks, state_rows, cache_k=None, cache_v=None):
    b, t, _ = h.shape
    kvd = SW_KV * SW_HD
    rep = SW_HEADS // SW_KV
    q, k, v, z = split_cols(h @ w_in, (SW_WIDTH, kvd, kvd, SW_WIDTH))
    q = partial_rope(q.reshape(b, t, SW_HEADS, SW_HD), pos)
    k = partial_rope(k.reshape(b, t, SW_KV, SW_HD), pos)
    v = v.reshape(b, t, SW_KV, SW_HD)
    scale = SW_HD ** -0.5
    if cache_k is None:
        lead = CHUNK - N_META
        n_prev = WINDOW // CHUNK
        kpad = lead + n_prev * CHUNK
        nc = (t + lead) // CHUNK
        qc = jnp.pad(q, ((0, 0), (lead, 0), (0, 0), (0, 0))).reshape(b, nc, CHUNK, SW_KV, rep, SW_HD)
        kc = jnp.pad(k, ((0, 0), (kpad, 0), (0, 0), (0, 0))).reshape(b, nc + n_prev, CHUNK, SW_KV, SW_HD)
        vc = jnp.pad(v, ((0, 0), (kpad, 0), (0, 0), (0, 0))).reshape(b, nc + n_prev, CHUNK, SW_KV, SW_HD)
        valid = jnp.pad(jnp.ones((t,), dtype=bool), (kpad, 0)).reshape(nc + n_prev, CHUNK)
        kb = jnp.concatenate([kc[:, j:j + nc] for j in range(n_prev + 1)], axis=2)
        vb = jnp.concatenate([vc[:, j:j + nc] for j in range(n_prev + 1)], axis=2)
        vmask = jnp.concatenate([valid[j:j + nc] for j in range(n_prev + 1)], axis=1)
        s = jnp.einsum('bcqgrd,bckgd->bcgrqk', qc, kb).astype(jnp.float32) * scale
        s = jnp.where(vmask[None, :, None, None, None, :], s, NEG_INF)
        o = sink_attend(s, vb, sinks, 'bcgrqk,bckgd->bcqgrd')
        o = o.reshape(b, nc * CHUNK, SW_WIDTH)[:, lead:]
        new_k = k[:, t - state_rows:]
        new_v = v[:, t - state_rows:]
    else:
        k_all = jnp.concatenate([cache_k, k], axis=1)
        v_all = jnp.concatenate([cache_v, v], axis=1)
        s = jnp.einsum('bqgrd,bkgd->bgrqk', q.reshape(b, t, SW_KV, rep, SW_HD), k_all).astype(jnp.float32) * scale
        o = sink_attend(s, v_all, sinks, 'bgrqk,bkgd->bqgrd').reshape(b, t, SW_WIDTH)
        new_k = k_all[:, k_all.shape[1] - state_rows:]
        new_v = v_all[:, v_all.shape[1] - state_rows:]
    o = o * jax.nn.silu(z)
    return o @ w_out, new_k, new_v


def dsa_select_attend(q, qi, wi, k, v, ki, topk, q_chunk, k_chunk):
    sc = jnp.einsum('bqhe,bse->bqhs', qi, ki).astype(jnp.float32)
    score = jnp.einsum('bqhs,bqh->bqs', jax.nn.relu(sc), wi.astype(jnp.float32))
    if q_chunk is not None:
        score = jnp.where(k_chunk[None, None, :] <= q_chunk[None, :, None], score, -jnp.inf)
    _, idx = lax.top_k(score, topk)
    gather = jax.vmap(lambda a, i: a[i])
    ks = gather(k, idx)
    vs = gather(v, idx)
    s = jnp.einsum('bqgrd,bqkgd->bqgrk', q, ks).astype(jnp.float32) * (DS_HD ** -0.5)
    if q_chunk is not None:
        ok = k_chunk[idx] <= q_chunk[None, :, None]
        s = jnp.where(ok[:, :, None, None, :], s, NEG_INF)
    p = jax.nn.softmax(s, axis=-1)
    return jnp.einsum('bqgrk,bqkgd->bqgrd', p.astype(v.dtype), vs)


def dsa_mixer(h, pos, w_in, w_out, cache_k=None, cache_v=None, cache_ki=None):
    b, t, _ = h.shape
    kvd = DS_KV * DS_HD
    rep = DS_HEADS // DS_KV
    q, k, v, z, qi, ki, wi = split_cols(
        h @ w_in, (DS_WIDTH, kvd, kvd, DS_WIDTH, IDX_HEADS * IDX_HD, IDX_HD, IDX_HEADS))
    q = partial_rope(q.reshape(b, t, DS_HEADS, DS_HD), pos).reshape(b, t, DS_KV, rep, DS_HD)
    k = partial_rope(k.reshape(b, t, DS_KV, DS_HD), pos)
    v = v.reshape(b, t, DS_KV, DS_HD)
    qi = partial_rope(qi.reshape(b, t, IDX_HEADS, IDX_HD), pos)
    ki = partial_rope(ki.reshape(b, t, 1, IDX_HD), pos)[:, :, 0]
    wi = wi * (IDX_HEADS ** -0.5 * IDX_HD ** -0.5)
    if cache_k is None:
        topk = min(TOPK_MAX, (t - N_META) // 4)
        tq = -(-t // Q_BLOCK) * Q_BLOCK
        k_chunk = prompt_chunk_ids(t)
        q_chunk = prompt_chunk_ids(tq).reshape(-1, Q_BLOCK)

        def block(args):
            qb, qib, wib, qc = args
            return dsa_select_attend(qb, qib, wib, k, v, ki, topk, qc, k_chunk)

        o = from_blocks(lax.map(block, (to_blocks(q, tq), to_blocks(qi, tq), to_blocks(wi, tq), q_chunk)), t)
    else:
        k_all = jnp.concatenate([cache_k, k], axis=1)
        v_all = jnp.concatenate([cache_v, v], axis=1)
        ki_all = jnp.concatenate([cache_ki, ki], axis=1)
        topk = min(TOPK_MAX, k_all.shape[1] // 4)
        o = dsa_select_attend(q, qi, wi, k_all, v_all, ki_all, topk, None, None)
    o = o.reshape(b, t, DS_WIDTH) * jax.nn.silu(z)
    return o @ w_out, k, v, ki


def setup_inputs(seed: int = 0) -> dict:
    key = jax.random.key(seed)
    ks = jax.random.split(key, 48)
    counter = [0]

    def nrm(shape, scale=1.0):
        kk = ks[counter[0]]
        counter[0] += 1
        return jax.random.normal(kk, shape, jnp.float32) * scale

    def gain(n):
        return 1.0 + nrm((n,), 0.02)

    sw_rows = min(WINDOW, PAST_LEN)
    d_in_a = 2 * DA_QK + 2 * DA_WIDTH
    d_in_b = 2 * SW_WIDTH + 2 * SW_KV * SW_HD
    d_in_c = 2 * DS_WIDTH + 2 * DS_KV * DS_HD + IDX_HEADS * IDX_HD + IDX_HD + IDX_HEADS
    din = D_MODEL ** -0.5
    x_prompt = nrm((BATCH, SEQ, D_MODEL))
    x_sample = nrm((DEC_BATCH, DEC_SEQ, D_MODEL))
    cache_l0_k = nrm((DEC_BATCH, PAST_LEN, DA_HEADS, 2 * DA_HD))
    cache_l0_v = nrm((DEC_BATCH, PAST_LEN, DA_HEADS, 2 * DA_HD))
    cache_l1_k = nrm((DEC_BATCH, sw_rows, SW_KV, SW_HD))
    cache_l1_v = nrm((DEC_BATCH, sw_rows, SW_KV, SW_HD))
    cache_l2_k = nrm((DEC_BATCH, PAST_LEN, DS_KV, DS_HD))
    cache_l2_v = nrm((DEC_BATCH, PAST_LEN, DS_KV, DS_HD))
    cache_l2_kidx = nrm((DEC_BATCH, PAST_LEN, IDX_HD))
    cache_l3_k = nrm((DEC_BATCH, PAST_LEN, DA_HEADS, 2 * DA_HD))
    cache_l3_v = nrm((DEC_BATCH, PAST_LEN, DA_HEADS, 2 * DA_HD))
    meta_tokens = nrm((N_META, D_MODEL))
    l0_norm = gain(D_MODEL)
    l0_w_in = nrm((D_MODEL, d_in_a), din)
    l0_w_out = nrm((DA_WIDTH, D_MODEL), DA_WIDTH ** -0.5)
    l0_lam_q1 = nrm((DA_HD,), 0.1)
    l0_lam_k1 = nrm((DA_HD,), 0.1)
    l0_lam_q2 = nrm((DA_HD,), 0.1)
    l0_lam_k2 = nrm((DA_HD,), 0.1)
    l0_subln = gain(2 * DA_HD)
    l1_norm = gain(D_MODEL)
    l1_w_in = nrm((D_MODEL, d_in_b), din)
    l1_w_out = nrm((SW_WIDTH, D_MODEL), SW_WIDTH ** -0.5)
    l1_sinks = nrm((SW_HEADS,), 0.5)
    l2_norm = gain(D_MODEL)
    l2_w_in = nrm((D_MODEL, d_in_c), din)
    l2_w_out = nrm((DS_WIDTH, D_MODEL), DS_WIDTH ** -0.5)
    l3_norm = gain(D_MODEL)
    l3_w_in = nrm((D_MODEL, d_in_a), din)
    l3_w_out = nrm((DA_WIDTH, D_MODEL), DA_WIDTH ** -0.5)
    l3_lam_q1 = nrm((DA_HD,), 0.1)
    l3_lam_k1 = nrm((DA_HD,), 0.1)
    l3_lam_q2 = nrm((DA_HD,), 0.1)
    l3_lam_k2 = nrm((DA_HD,), 0.1)
    l3_subln = gain(2 * DA_HD)
    final_norm = gain(D_MODEL)
    return {
        'x_prompt': x_prompt, 'x_sample': x_sample,
        'cache_l0_k': cache_l0_k, 'cache_l0_v': cache_l0_v,
        'cache_l1_k': cache_l1_k, 'cache_l1_v': cache_l1_v,
        'cache_l2_k': cache_l2_k, 'cache_l2_v': cache_l2_v, 'cache_l2_kidx': cache_l2_kidx,
        'cache_l3_k': cache_l3_k, 'cache_l3_v': cache_l3_v,
        'meta_tokens': meta_tokens,
        'l0_norm': l0_norm, 'l0_w_in': l0_w_in, 'l0_w_out': l0_w_out,
        'l0_lam_q1': l0_lam_q1, 'l0_lam_k1': l0_lam_k1, 'l0_lam_q2': l0_lam_q2, 'l0_lam_k2': l0_lam_k2,
        'l0_subln': l0_subln,
        'l1_norm': l1_norm, 'l1_w_in': l1_w_in, 'l1_w_out': l1_w_out, 'l1_sinks': l1_sinks,
        'l2_norm': l2_norm, 'l2_w_in': l2_w_in, 'l2_w_out': l2_w_out,
        'l3_norm': l3_norm, 'l3_w_in': l3_w_in, 'l3_w_out': l3_w_out,
        'l3_lam_q1': l3_lam_q1, 'l3_lam_k1': l3_lam_k1, 'l3_lam_q2': l3_lam_q2, 'l3_lam_k2': l3_lam_k2,
        'l3_subln': l3_subln,
        'final_norm': final_norm,
    }


def reference(x_prompt, x_sample, cache_l0_k, cache_l0_v, cache_l1_k, cache_l1_v,
              cache_l2_k, cache_l2_v, cache_l2_kidx, cache_l3_k, cache_l3_v, meta_tokens,
              l0_norm, l0_w_in, l0_w_out, l0_lam_q1, l0_lam_k1, l0_lam_q2, l0_lam_k2, l0_subln,
              l1_norm, l1_w_in, l1_w_out, l1_sinks,
              l2_norm, l2_w_in, l2_w_out,
              l3_norm, l3_w_in, l3_w_out, l3_lam_q1, l3_lam_k1, l3_lam_q2, l3_lam_k2, l3_subln,
              final_norm):
    b = x_prompt.shape[0]
    t = N_META + x_prompt.shape[1]
    meta = jnp.broadcast_to(meta_tokens.astype(x_prompt.dtype)[None], (b, N_META, D_MODEL))
    xp = jnp.concatenate([meta, x_prompt], axis=1)
    xs = x_sample
    pos_p = jnp.arange(t)
    pos_s = cache_l0_k.shape[1] + jnp.arange(xs.shape[1])
    layers = [
        dict(norm=l0_norm, w_in=l0_w_in, w_out=l0_w_out, lam=(l0_lam_q1, l0_lam_k1, l0_lam_q2, l0_lam_k2),
             subln=l0_subln, cache=(cache_l0_k, cache_l0_v)),
        dict(norm=l1_norm, w_in=l1_w_in, w_out=l1_w_out, sinks=l1_sinks, cache=(cache_l1_k, cache_l1_v)),
        dict(norm=l2_norm, w_in=l2_w_in, w_out=l2_w_out, cache=(cache_l2_k, cache_l2_v, cache_l2_kidx)),
        dict(norm=l3_norm, w_in=l3_w_in, w_out=l3_w_out, lam=(l3_lam_q1, l3_lam_k1, l3_lam_q2, l3_lam_k2),
             subln=l3_subln, cache=(cache_l3_k, cache_l3_v)),
    ]
    p_st = []
    s_st = []
    for i in range(DEPTH):
        lp = layers[i]
        hp = rms_norm(xp, lp['norm'])
        hs = rms_norm(xs, lp['norm'])
        kind = i % N_MIXERS
        if kind == 0:
            lam_init = 0.8 - 0.6 * math.exp(-0.3 * i)
            args = (lp['w_in'], lp['w_out'], *lp['lam'], lp['subln'], lam_init)
            op, *sp = diff_mixer(hp, pos_p, *args)
            os_, *ss = diff_mixer(hs, pos_s, *args, *lp['cache'])
        elif kind == 1:
            rows = lp['cache'][0].shape[1]
            op, *sp = swa_mixer(hp, pos_p, lp['w_in'], lp['w_out'], lp['sinks'], rows)
            os_, *ss = swa_mixer(hs, pos_s, lp['w_in'], lp['w_out'], lp['sinks'], rows, *lp['cache'])
        else:
            op, *sp = dsa_mixer(hp, pos_p, lp['w_in'], lp['w_out'])
            os_, *ss = dsa_mixer(hs, pos_s, lp['w_in'], lp['w_out'], *lp['cache'])
        xp = xp + op
        xs = xs + os_
        p_st.append(sp)
        s_st.append(ss)
    y_prompt = rms_norm(xp, final_norm)[:, N_META:]
    y_sample = rms_norm(xs, final_norm)
    return (y_prompt, y_sample,
            p_st[0][0], p_st[0][1], s_st[0][0], s_st[0][1],
            p_st[1][0], p_st[1][1], s_st[1][0], s_st[1][1],
            p_st[2][0], p_st[2][1], p_st[2][2], s_st[2][0], s_st[2][1], s_st[2][2],
            p_st[3][0], p_st[3][1], s_st[3][0], s_st[3][1])
```

```python
import numpy as np
import ml_dtypes
from contextlib import ExitStack
import concourse.bass as bass
import concourse.mybir as mybir
from concourse.bass_utils import run_bass_kernel_spmd
from concourse.alu_op_type import AluOpType as ALU

AF = mybir.ActivationFunctionType
F32 = mybir.dt.float32
BF16 = mybir.dt.bfloat16
NEGBIG = -3.0e38
EPS = 1e-6


class Sem:
    def __init__(self, h, i):
        self.h = h
        self.i = i
        self.total = 0


class Res:
    __slots__ = ("name", "w", "r")

    def __init__(self, name):
        self.name = name
        self.w = None
        self.r = {}


class Prog:
    def __init__(self, nc, stack):
        self.nc = nc
        self.stack = stack
        self.eng = dict(pe=nc.tensor, act=nc.scalar, dve=nc.vector, pool=nc.gpsimd, sp=nc.sync)
        self.nsem = 0
        self.dsems = []
        self.esem = {k: self._new_sem("e_" + k) for k in self.eng}
        self.ecnt = {k: 0 for k in self.eng}
        self.known = {k: {} for k in self.eng}
        self.nwait = 0
        self.ninst = 0

    def _new_sem(self, name):
        h = self.stack.enter_context(self.nc.semaphore(name))
        self.nsem += 1
        return Sem(h, self.nsem)

    def new_dsem(self, name):
        if getattr(self, "free_dsems", None):
            return self.free_dsems.pop()
        s = self._new_sem(name)
        self.dsems.append(s)
        return s

    def release_dsem(self, s):
        if not hasattr(self, "free_dsems"):
            self.free_dsems = []
        self.free_dsems.append(s)

    def _wait(self, e, ev):
        if ev is None:
            return
        sem, val = ev
        if val <= 0:
            return
        if e == "pe" and sem is self.esem["pe"]:
            return
        kn = self.known[e]
        if kn.get(sem.i, 0) >= val:
            return
        self.eng[e].wait_ge(sem.h, val)
        kn[sem.i] = val
        self.nwait += 1

    def _deps(self, e, reads, writes):
        for r in reads:
            self._wait(e, r.w)
        for w in writes:
            self._wait(e, w.w)
            for ev in list(w.r.values()):
                self._wait(e, ev)

    def _mark(self, ev, reads, writes):
        for r in reads:
            old = r.r.get(ev[0].i)
            if old is None or old[1] < ev[1]:
                r.r[ev[0].i] = ev
        for w in writes:
            w.w = ev
            w.r = {}

    def op(self, e, fn, reads=(), writes=(), inc=True):
        self._deps(e, reads, writes)
        inst = fn(self.eng[e])
        self.ninst += 1
        if inc:
            self.ecnt[e] += 1
            inst.then_inc(self.esem[e].h, 1)
            ev = (self.esem[e], self.ecnt[e])
        else:
            ev = (self.esem[e], self.ecnt[e] + 1)
        self._mark(ev, reads, writes)
        return inst

    def dma(self, q, dsem, pairs, reads=(), writes=(), **kw):
        self._deps(q, reads, writes)
        self._wait(q, (dsem, dsem.total))
        for (o, i) in pairs:
            self.eng[q].dma_start(out=o, in_=i, **kw).then_inc(dsem.h, 16)
            dsem.total += 16
            self.ninst += 1
        ev = (dsem, dsem.total)
        self._mark(ev, reads, writes)

    def barrier(self):
        for e in self.eng:
            for o in self.eng:
                if o != e:
                    self._wait(e, (self.esem[o], self.ecnt[o]))
            for d in self.dsems:
                self._wait(e, (d, d.total))


class T:
    _n = [0]

    def __init__(self, P, st, name, shape, dt, psum=False, dma=True):
        T._n[0] += 1
        name = f"t{T._n[0]}_{name}"
        if psum:
            self.t = st.enter_context(P.nc.psum_tensor(name, list(shape), dt))
        else:
            self.t = st.enter_context(P.nc.sbuf_tensor(name, list(shape), dt))
        self.r = Res(name)
        self.d = P.new_dsem("d_" + name) if (dma and not psum) else None
        if self.d is not None and st is not P.stack:
            st.callback(P.release_dsem, self.d)

    def __getitem__(self, k):
        return self.t[k]


class Ring:
    def __init__(self, P, st, name, shape, dt, n, psum=False, dma=True):
        self.tiles = [T(P, st, f"{name}{i}", shape, dt, psum=psum, dma=dma) for i in range(n)]
        self.i = 0

    def next(self):
        t = self.tiles[self.i % len(self.tiles)]
        self.i += 1
        return t


class Cfg:
    def __init__(self, SEQ, PAST, nlayers=4):
        self.SEQ = SEQ
        self.PAST = PAST
        self.T = 16 + SEQ
        self.NT = 1 + SEQ // 128
        self.U = self.NT * 128
        self.NTT = self.NT + 1
        self.UT = self.NTT * 128
        self.NPT = PAST // 128
        self.topk_p = min(256, (self.T - 16) // 4)
        self.topk_s = min(256, (PAST + 64) // 4)
        self.nlayers = nlayers
        self.G = 11


LAYER_KIND = ["A", "B", "C", "A"]
DIN = {"A": 8192, "B": 4608, "C": 6224}
LAM_INIT = [0.8 - 0.6 * float(np.exp(-0.3 * i)) for i in range(4)]


def layer_blocks(kind):
    B = []
    if kind == "A":
        for b in range(4):
            B.append((b * 512, 512, [(0, 512, "rope", dict(hd=128, dup=False, out=None, tdst="qT", tbase=4 * b))]))
        for b in range(4):
            B.append((2048 + b * 512, 512, [(0, 512, "rope", dict(hd=128, dup=False, out=("k", b * 512), tdst="kT", tbase=4 * b))]))
        for b in range(4):
            B.append((4096 + b * 512, 512, [(0, 512, "v", dict(out=("v", b * 512), vcol=b * 512))]))
        for b in range(4):
            B.append((6144 + b * 512, 512, [(0, 512, "z", dict(zcol=b * 512))]))
    elif kind == "B":
        for b in range(4):
            B.append((b * 512, 512, [(0, 512, "rope", dict(hd=64, dup=False, out=None, tdst="qT", tbase=4 * b))]))
        B.append((2048, 512, [(0, 256, "rope", dict(hd=64, dup=True, out=("k", 0), tdst="kT", tbase=0)),
                              (256, 256, "v", dict(out=("v", 0), vcol=0))]))
        for b in range(4):
            B.append((2560 + b * 512, 512, [(0, 512, "z", dict(zcol=b * 512))]))
    else:
        for b in range(4):
            B.append((b * 512, 512, [(0, 512, "rope", dict(hd=128, dup=False, out=None, tdst="qT", tbase=4 * b))]))
        B.append((2048, 512, [(0, 512, "rope", dict(hd=128, dup=False, out=("k", 0), tdst="kT", tbase=0))]))
        B.append((2560, 512, [(0, 512, "v", dict(out=("v", 0), vcol=0))]))
        for b in range(4):
            B.append((3072 + b * 512, 512, [(0, 512, "z", dict(zcol=b * 512))]))
        for b in range(2):
            B.append((5120 + b * 512, 512, [(0, 512, "rope", dict(hd=64, dup=False, out=None, tdst="qiT", tbase=4 * b))]))
        B.append((6144, 80, [(0, 64, "rope", dict(hd=64, dup=True, out=("i", 0), tdst="kiT", tbase=0)),
                             (64, 16, "wi", dict())]))
    return B


def build(cfg):
    nc = bass.Bass("TRN2", target_bir_lowering=False)
    NT, NTT, U, UT, SEQ, PAST, Tn, NPT = cfg.NT, cfg.NTT, cfg.U, cfg.UT, cfg.SEQ, cfg.PAST, cfg.T, cfg.NPT
    D = {}

    def din(name, shape, dt=F32):
        D[name] = nc.dram_tensor(name, list(shape), dt, kind="ExternalInput").ap()

    def dout(name, shape, dt=F32):
        D[name] = nc.dram_tensor(name, list(shape), dt, kind="ExternalOutput").ap()

    def dscr(name, shape, dt):
        D[name] = nc.dram_tensor(name, list(shape), dt, kind="Internal").ap()

    din("xp", [SEQ, 2048]); din("xs", [128, 2048]); din("meta", [16, 2048])
    for l in (0, 3):
        din(f"c{l}k", [2, PAST, 2048]); din(f"c{l}v", [2, PAST, 2048])
    din("c1k", [2, 128, 256]); din("c1v", [2, 128, 256])
    din("c2k", [2, PAST, 512]); din("c2v", [2, PAST, 512]); din("c2i", [2, PAST, 64])
    for l in range(4):
        din(f"normv{l}", [2048]); din(f"win{l}", [2048, DIN[LAYER_KIND[l]]]); din(f"wout{l}", [2048, 2048])
    for l in (0, 3):
        din(f"lam{l}", [128, 4]); din(f"subln{l}", [256])
    din("sinks", [32]); din("fnorm", [2048])
    din("ident", [128, 128], BF16); din("cs128", [UT, 32]); din("cs64", [UT, 16])
    din("dmask", [128, 128], BF16); din("swm", [2, 128, 128], BF16)

    dout("yp", [SEQ, 2048]); dout("ys", [128, 2048])
    for l in (0, 3):
        dout(f"p{l}k", [Tn, 2048]); dout(f"p{l}v", [Tn, 2048]); dout(f"s{l}k", [128, 2048]); dout(f"s{l}v", [128, 2048])
    dout("p1k", [128, 256]); dout("p1v", [128, 256]); dout("s1k", [2, 128, 256]); dout("s1v", [2, 128, 256])
    dout("p2k", [Tn, 512]); dout("p2v", [Tn, 512]); dout("p2i", [Tn, 64])
    dout("s2k", [128, 512]); dout("s2v", [128, 512]); dout("s2i", [128, 64])

    dscr("resid", [UT, 2048], F32)
    dscr("qT", [16, 128, UT], BF16); dscr("kT", [16, 128, UT], BF16)
    dscr("vS", [UT, 2048], BF16); dscr("zS", [UT, 2048], BF16); dscr("ogS", [UT, 2048], BF16)
    dscr("qiT", [8, 128, UT], BF16); dscr("kiT", [1, 128, UT], BF16); dscr("wiS", [UT, 16], F32)
    dscr("mkT", [NT, 128, NT, 128], BF16)
    dscr("mkS", [2, 128, NPT + 1, 64], BF16)
    RD = {k: Res("dram_" + k) for k in D}

    out_written = []

    with ExitStack() as gst:
        P = Prog(nc, gst)
        ident = T(P, gst, "ident", [128, 128], BF16)
        cs128 = T(P, gst, "cs128", [128, NTT, 32], F32)
        cs64 = T(P, gst, "cs64", [128, NTT, 16], F32)
        dmask = T(P, gst, "dmask", [128, 128], BF16)
        swm = T(P, gst, "swm", [128, 2, 128], BF16)
        P.dma("sp", ident.d, [(ident[:], D["ident"])], writes=[ident.r])
        P.dma("sp", cs128.d, [(cs128[:], D["cs128"].rearrange("(t p) c -> p t c", p=128))], writes=[cs128.r])
        P.dma("sp", cs64.d, [(cs64[:], D["cs64"].rearrange("(t p) c -> p t c", p=128))], writes=[cs64.r])
        P.dma("sp", dmask.d, [(dmask[:], D["dmask"])], writes=[dmask.r])
        P.dma("sp", swm.d, [(swm[:], D["swm"].rearrange("j p c -> p j c"))], writes=[swm.r])


        def tile_rows(t):
            return slice(t * 128, (t + 1) * 128)

        def out_rows(t):
            if t == 0:
                return [(slice(112, 128), slice(0, 16))]
            if t < NT:
                return [(slice(0, 128), slice(16 + 128 * (t - 1), 16 + 128 * t))]
            return [(slice(0, 128), slice(0, 128))]

        def inproj(l):
            kind = LAYER_KIND[l]
            blocks = layer_blocks(kind)
            win = D[f"win{l}"]
            with ExitStack() as st:
                G = cfg.G
                xnT = T(P, st, "xnT", [128, 16, G * 128], BF16, dma=False)
                xin = Ring(P, st, "xin", [128, 2048], F32, 2)
                xnb = Ring(P, st, "xnb", [128, 2048], BF16, 2, dma=False)
                junk = T(P, st, "junk", [128, 2048], BF16, dma=False)
                ssr = Ring(P, st, "ss", [128, 1], F32, 2, dma=False)
                wst = Ring(P, st, "wst", [128, 8, 512], F32, 2)
                wbr = Ring(P, st, "wb", [128, 16, 512], BF16, 2, dma=False)
                blkr = Ring(P, st, "blk", [128, 512], F32, 3)
                blkbr = Ring(P, st, "blkb", [128, 512], BF16, 3)
                stg = Ring(P, st, "stg", [128, 4, 128], BF16, 3)
                rtA = T(P, st, "rtA", [128, 64], F32, dma=False)
                rtB = T(P, st, "rtB", [128, 64], F32, dma=False)
                rtC = T(P, st, "rtC", [128, 64], F32, dma=False)
                rtD = T(P, st, "rtD", [128, 64], F32, dma=False)
                tpr = Ring(P, st, "tp", [128, 8, 128], BF16, 2, psum=True)
                mmr = Ring(P, st, "mm", [128, 512], F32, 2, psum=True)
                tqr = Ring(P, st, "tq", [128, 8, 128], BF16, 2, psum=True)
                gbc = T(P, st, "gbc", [128, 2048], F32)
                P.dma("sp", gbc.d, [(gbc[:], D[f"normv{l}"].partition_broadcast(128))], writes=[gbc.r])

                def load_x(t, xt):
                    if l == 0:
                        if t == 0:
                            P.op("pool", lambda e: e.memset(xt[:], 0.0), writes=[xt.r])
                            P.dma("sp", xt.d, [(xt[112:128, :], D["meta"])], writes=[xt.r])
                        elif t < NT:
                            P.dma("sp", xt.d, [(xt[:], D["xp"][(t - 1) * 128:t * 128, :])], writes=[xt.r])
                        else:
                            P.dma("sp", xt.d, [(xt[:], D["xs"])], writes=[xt.r])
                        P.dma("sp", xt.d, [(D["resid"][tile_rows(t), :], xt[:])], reads=[xt.r], writes=[RD["resid"]])
                    else:
                        P.dma("sp", xt.d, [(xt[:], D["resid"][tile_rows(t), :])], reads=[RD["resid"]], writes=[xt.r])

                def seg_rope(t, ps, c0, n, o):
                    hd = o["hd"]; nh = n // hd; half = hd // 8
                    cs = cs128 if hd == 128 else cs64
                    cosb = cs[:, t, 0:half].unsqueeze(1).broadcast_to([128, nh, half])
                    sinb = cs[:, t, half:2 * half].unsqueeze(1).broadcast_to([128, nh, half])
                    blk = blkr.next()
                    ps3 = ps[:, c0:c0 + n].rearrange("p (h d) -> p h d", d=hd)
                    b3 = blk[:, 0:n].rearrange("p (h d) -> p h d", d=hd)
                    A3 = rtA[:, 0:nh * half].rearrange("p (h d) -> p h d", d=half)
                    B3 = rtB[:, 0:nh * half].rearrange("p (h d) -> p h d", d=half)
                    P.op("act", lambda e: e.copy(out=blk[:, 0:n], in_=ps[:, c0:c0 + n]), reads=[ps.r], writes=[blk.r])
                    x1 = b3[:, :, 0:half]; x2 = b3[:, :, half:2 * half]
                    C3 = rtC[:, 0:nh * half].rearrange("p (h d) -> p h d", d=half)
                    D3 = rtD[:, 0:nh * half].rearrange("p (h d) -> p h d", d=half)
                    KR = 7
                    P.op("dve", lambda e: e.tensor_tensor(out=A3, in0=x1, in1=cosb, op=ALU.mult), reads=[blk.r, cs.r], writes=[rtA.r])
                    P.op("dve", lambda e: e.tensor_tensor(out=B3, in0=x2, in1=sinb, op=ALU.mult), reads=[blk.r, cs.r], writes=[rtB.r])
                    P.op("dve", lambda e: e.tensor_tensor(out=C3, in0=x2, in1=cosb, op=ALU.mult), reads=[blk.r, cs.r], writes=[rtC.r])
                    P.op("dve", lambda e: e.tensor_tensor(out=D3, in0=x1, in1=sinb, op=ALU.mult), reads=[blk.r, cs.r], writes=[rtD.r])
                    P.op("dve", lambda e: e.tensor_tensor(out=x1, in0=A3, in1=B3, op=ALU.subtract), reads=[rtA.r, rtB.r], writes=[blk.r])
                    P.op("dve", lambda e: e.tensor_tensor(out=x2, in0=C3, in1=D3, op=ALU.add), reads=[rtC.r, rtD.r], writes=[blk.r])
                    if o["out"] is not None and (KR & 2):
                        which, col = o["out"]
                        store_out(l, which, t, blk, n, col)
                    if not (KR & 4):
                        return
                    bb = blkbr.next()
                    if o["dup"]:
                        bd = bb[:, 0:2 * n].rearrange("p (h two d) -> p h two d", two=2, d=hd)
                        for j in range(2):
                            P.op("pool", lambda e, j=j: e.tensor_copy(out=bd[:, :, j, :], in_=b3), reads=[blk.r], writes=[bb.r])
                        ncol = 2 * n
                    else:
                        P.op("pool", lambda e: e.tensor_copy(out=bb[:, 0:n], in_=blk[:, 0:n]), reads=[blk.r], writes=[bb.r])
                        ncol = n
                    ntr = ncol // 128
                    tq = tqr.next()
                    for i in range(ntr):
                        P.op("pe", lambda e, i=i: e.transpose(out=tq[:, i, :], in_=bb[:, i * 128:(i + 1) * 128], identity=ident[:]),
                             reads=[bb.r, ident.r], writes=[tq.r], inc=(i == ntr - 1))
                    sg = stg.next()
                    P.op("act", lambda e: e.copy(out=sg[:, 0:ntr, :], in_=tq[:, 0:ntr, :]), reads=[tq.r], writes=[sg.r])
                    dst = D[o["tdst"]][o["tbase"]:o["tbase"] + ntr, :, tile_rows(t)].rearrange("i p c -> p i c")
                    P.dma("sp", sg.d, [(dst, sg[:, 0:ntr, :])], reads=[sg.r], writes=[RD[o["tdst"]]])

                def store_out(l, which, t, blk, n, col):
                    if kind == "B":
                        nm = {"k": "1k", "v": "1v"}[which]
                        if t == NT - 1:
                            P.dma("sp", blk.d, [(D["p" + nm][:, :], blk[:, 0:n])], reads=[blk.r], writes=[RD["p" + nm]])
                        elif t == NT:
                            P.dma("sp", blk.d, [(D["s" + nm][b, 64:128, :], blk[b * 64:(b + 1) * 64, 0:n]) for b in range(2)],
                                  reads=[blk.r], writes=[RD["s" + nm]])
                        return
                    nm = f"{l}{which}"
                    dn = ("p" if t < NT else "s") + nm
                    P.dma("sp", blk.d, [(D[dn][rs, col:col + n], blk[ps_, 0:n]) for (ps_, rs) in out_rows(t)],
                          reads=[blk.r], writes=[RD[dn]])

                def seg_v(t, ps, c0, n, o):
                    blk = blkr.next()
                    P.op("act", lambda e: e.copy(out=blk[:, 0:n], in_=ps[:, c0:c0 + n]), reads=[ps.r], writes=[blk.r])
                    which, col = o["out"]
                    store_out(l, which, t, blk, n, col)
                    bb = blkbr.next()
                    P.op("pool", lambda e: e.tensor_copy(out=bb[:, 0:n], in_=blk[:, 0:n]), reads=[blk.r], writes=[bb.r])
                    P.dma("sp", bb.d, [(D["vS"][tile_rows(t), o["vcol"]:o["vcol"] + n], bb[:, 0:n])], reads=[bb.r], writes=[RD["vS"]])

                def seg_z(t, ps, c0, n, o):
                    bb = blkbr.next()
                    P.op("act", lambda e: e.activation(out=bb[:, 0:n], in_=ps[:, c0:c0 + n], func=AF.Silu), reads=[ps.r], writes=[bb.r])
                    P.dma("sp", bb.d, [(D["zS"][tile_rows(t), o["zcol"]:o["zcol"] + n], bb[:, 0:n])], reads=[bb.r], writes=[RD["zS"]])

                def seg_wi(t, ps, c0, n, o):
                    blk = blkr.next()
                    P.op("act", lambda e: e.copy(out=blk[:, 0:n], in_=ps[:, c0:c0 + n]), reads=[ps.r], writes=[blk.r])
                    P.dma("sp", blk.d, [(D["wiS"][tile_rows(t), :], blk[:, 0:n])], reads=[blk.r], writes=[RD["wiS"]])

                SEG = dict(rope=seg_rope, v=seg_v, z=seg_z, wi=seg_wi)

                tiles = list(range(NTT))
                groups = [tiles[i:i + G] for i in range(0, NTT, G)]
                for grp in groups:
                    for s, t in enumerate(grp):
                        xt = xin.next(); ss = ssr.next(); xn = xnb.next()
                        load_x(t, xt)
                        P.op("act", lambda e: e.activation(out=junk[:], in_=xt[:], func=AF.Square, accum_out=ss[:]),
                             reads=[xt.r], writes=[junk.r, ss.r])
                        P.op("dve", lambda e: e.tensor_scalar(out=ss[:], in0=ss[:], scalar1=1.0 / 2048, scalar2=EPS, op0=ALU.mult, op1=ALU.add),
                             reads=[ss.r], writes=[ss.r])
                        P.op("act", lambda e: e.activation(out=ss[:], in_=ss[:], func=AF.Sqrt), reads=[ss.r], writes=[ss.r])
                        P.op("dve", lambda e: e.reciprocal(out=ss[:], in_=ss[:]), reads=[ss.r], writes=[ss.r])
                        P.op("dve", lambda e: e.scalar_tensor_tensor(out=xn[:], in0=xt[:], scalar=ss[:], in1=gbc[:], op0=ALU.mult, op1=ALU.mult),
                             reads=[xt.r, ss.r, gbc.r], writes=[xn.r])
                        for hf in range(2):
                            tp = tpr.next()
                            for i in range(8):
                                kc = hf * 8 + i
                                P.op("pe", lambda e, i=i, kc=kc: e.transpose(out=tp[:, i, :], in_=xn[:, kc * 128:(kc + 1) * 128], identity=ident[:]),
                                     reads=[xn.r, ident.r], writes=[tp.r], inc=(i == 7))
                            P.op("dve", lambda e: e.tensor_copy(out=xnT[:, hf * 8:hf * 8 + 8, s * 128:(s + 1) * 128], in_=tp[:]),
                                 reads=[tp.r], writes=[xnT.r])
                    import os
                    KSUB = int(os.environ.get("KSUB", "99"))
                    if KSUB == 1:
                        continue
                    for (c0, ncb, segs) in blocks:
                        wb = wbr.next()
                        for hf in range(2):
                            ws = wst.next()
                            src = win[hf * 1024:(hf + 1) * 1024, c0:c0 + ncb].rearrange("(k p) n -> p k n", p=128)
                            P.dma("sp", ws.d, [(ws[:, :, 0:ncb], src)], writes=[ws.r])
                            P.op("pool", lambda e: e.tensor_copy(out=wb[:, hf * 8:hf * 8 + 8, 0:ncb], in_=ws[:, :, 0:ncb]),
                                 reads=[ws.r], writes=[wb.r])
                        for s, t in enumerate(grp):
                            ps = mmr.next()
                            for kc in range(16):
                                P.op("pe", lambda e, kc=kc: e.matmul(ps[:, 0:ncb], lhsT=xnT[:, kc, s * 128:(s + 1) * 128], rhs=wb[:, kc, 0:ncb],
                                                                    start=(kc == 0), stop=(kc == 15)),
                                     reads=[xnT.r, wb.r], writes=[ps.r], inc=(kc == 15))
                            for (sc0, sn, sk, so) in segs:
                                if KSUB == 2 or (KSUB == 3 and sk == "rope"):
                                    continue
                                SEG[sk](t, ps, sc0, sn, so)
                P.barrier()
            if kind == "B":
                with ExitStack() as st2:
                    for nm in ("k", "v"):
                        cw = T(P, st2, "cw" + nm, [128, 256], F32)
                        P.dma("sp", cw.d, [(cw[b * 64:(b + 1) * 64, :], D["c1" + nm][b, 64:128, :]) for b in range(2)], writes=[cw.r])
                        P.dma("sp", cw.d, [(D["s1" + nm][b, 0:64, :], cw[b * 64:(b + 1) * 64, :]) for b in range(2)], reads=[cw.r], writes=[RD["s1" + nm]])
                    P.barrier()

        def outproj(l, last):
            with ExitStack() as st:
                wo = T(P, st, "wo", [128, 16, 2048], BF16, dma=False)
                wst = Ring(P, st, "wst", [128, 8, 512], F32, 2)
                ogr = Ring(P, st, "og", [128, 2048], BF16, 2)
                ogT = Ring(P, st, "ogT", [128, 16, 128], BF16, 2, dma=False)
                xin = Ring(P, st, "xin", [128, 2048], F32, 2)
                xo = Ring(P, st, "xo", [128, 2048], F32, 2)
                yo = Ring(P, st, "yo", [128, 2048], F32, 2)
                junk = T(P, st, "junk", [128, 2048], BF16, dma=False)
                ssr = Ring(P, st, "ss", [128, 1], F32, 2, dma=False)
                gfin = T(P, st, "gfin", [128, 2048], F32)
                tpr = Ring(P, st, "tp", [128, 8, 128], BF16, 2, psum=True)
                mmr = Ring(P, st, "mm", [128, 512], F32, 2, psum=True)
                wout = D[f"wout{l}"]
                if last:
                    P.dma("sp", gfin.d, [(gfin[:], D["fnorm"].partition_broadcast(128))], writes=[gfin.r])
                for cb in range(4):
                    for hf in range(2):
                        ws = wst.next()
                        src = wout[hf * 1024:(hf + 1) * 1024, cb * 512:(cb + 1) * 512].rearrange("(k p) n -> p k n", p=128)
                        P.dma("sp", ws.d, [(ws[:], src)], writes=[ws.r])
                        P.op("pool", lambda e: e.tensor_copy(out=wo[:, hf * 8:hf * 8 + 8, cb * 512:(cb + 1) * 512], in_=ws[:]),
                             reads=[ws.r], writes=[wo.r])
                for t in range(NTT):
                    og = ogr.next(); oT = ogT.next(); xt = xin.next(); xn = xo.next()
                    P.dma("sp", og.d, [(og[:], D["ogS"][tile_rows(t), :])], reads=[RD["ogS"]], writes=[og.r])
                    P.dma("sp", xt.d, [(xt[:], D["resid"][tile_rows(t), :])], reads=[RD["resid"]], writes=[xt.r])
                    for hf in range(2):
                        tp = tpr.next()
                        for i in range(8):
                            kc = hf * 8 + i
                            P.op("pe", lambda e, i=i, kc=kc: e.transpose(out=tp[:, i, :], in_=og[:, kc * 128:(kc + 1) * 128], identity=ident[:]),
                                 reads=[og.r, ident.r], writes=[tp.r], inc=(i == 7))
                        P.op("act", lambda e: e.copy(out=oT[:, hf * 8:hf * 8 + 8, :], in_=tp[:]), reads=[tp.r], writes=[oT.r])
                    for cb in range(4):
                        ps = mmr.next()
                        for kc in range(16):
                            P.op("pe", lambda e, kc=kc: e.matmul(ps[:], lhsT=oT[:, kc, :], rhs=wo[:, kc, cb * 512:(cb + 1) * 512],
                                                                start=(kc == 0), stop=(kc == 15)),
                                 reads=[oT.r, wo.r], writes=[ps.r], inc=(kc == 15))
                        P.op("dve", lambda e: e.tensor_tensor(out=xn[:, cb * 512:(cb + 1) * 512], in0=ps[:], in1=xt[:, cb * 512:(cb + 1) * 512], op=ALU.add),
                             reads=[ps.r, xt.r], writes=[xn.r])
                    if t == 0:
                        P.op("pool", lambda e: e.memset(xn[0:112, :], 0.0), writes=[xn.r])
                    if not last:
                        P.dma("sp", xn.d, [(D["resid"][tile_rows(t), :], xn[:])], reads=[xn.r], writes=[RD["resid"]])
                    else:
                        ss = ssr.next(); y = yo.next()
                        P.op("act", lambda e: e.activation(out=junk[:], in_=xn[:], func=AF.Square, accum_out=ss[:]),
                             reads=[xn.r], writes=[junk.r, ss.r])
                        P.op("dve", lambda e: e.tensor_scalar(out=ss[:], in0=ss[:], scalar1=1.0 / 2048, scalar2=EPS, op0=ALU.mult, op1=ALU.add),
                             reads=[ss.r], writes=[ss.r])
                        P.op("act", lambda e: e.activation(out=ss[:], in_=ss[:], func=AF.Sqrt), reads=[ss.r], writes=[ss.r])
                        P.op("dve", lambda e: e.reciprocal(out=ss[:], in_=ss[:]), reads=[ss.r], writes=[ss.r])
                        P.op("dve", lambda e: e.scalar_tensor_tensor(out=y[:], in0=xn[:], scalar=ss[:], in1=gfin[:], op0=ALU.mult, op1=ALU.mult),
                             reads=[xn.r, ss.r, gfin.r], writes=[y.r])
                        if 1 <= t < NT:
                            P.dma("sp", y.d, [(D["yp"][(t - 1) * 128:t * 128, :], y[:])], reads=[y.r], writes=[RD["yp"]])
                        elif t == NT:
                            P.dma("sp", y.d, [(D["ys"][:, :], y[:])], reads=[y.r], writes=[RD["ys"]])
                P.barrier()

        def attn_diff(l):
            scale = 128 ** -0.5
            with ExitStack() as st:
                kts = T(P, st, "kts", [128, 2, U], BF16)
                vaug = T(P, st, "vaug", [128, NT, 257], BF16)
                qtr = Ring(P, st, "qts", [128, 2, 512], BF16, 2)
                ztr = Ring(P, st, "zt", [128, 4, 256], BF16, 2)
                ptr = Ring(P, st, "pt", [128, 512], BF16, 3, dma=False)
                r1 = T(P, st, "r1", [128, 4, 256], F32, dma=False)
                res = T(P, st, "res", [128, 4, 256], F32, dma=False)
                rec = Ring(P, st, "rec", [128, 1], F32, 4, dma=False)
                ssr = Ring(P, st, "ss", [128, 1], F32, 4, dma=False)
                junk = T(P, st, "junk", [128, 256], BF16, dma=False)
                gz = Ring(P, st, "gz", [128, 256], F32, 2, dma=False)
                ogb = Ring(P, st, "ogb", [128, 256], BF16, 3)
                g256 = T(P, st, "g256", [128, 256], F32)
                lamt = T(P, st, "lamt", [128, 4], F32)
                lam2 = T(P, st, "lam2", [128, 2], F32, dma=False)
                nlam = T(P, st, "nlam", [128, 1], F32, dma=False)
                ones = T(P, st, "ones", [128, 128], F32, dma=False)
                cst = Ring(P, st, "cst", [128, NPT, 256], F32, 2)
                cbf = T(P, st, "cbf", [128, NPT, 256], BF16, dma=False)
                ktc = T(P, st, "ktc", [128, 2, PAST + 64], BF16)
                vac = T(P, st, "vac", [128, NPT + 1, 257], BF16)
                qsm = T(P, st, "qsm", [128, 2, 64], BF16)
                zsm = T(P, st, "zsm", [64, 256], BF16)
                str_ = Ring(P, st, "st", [128, 512], F32, 2, psum=True)
                accr = [T(P, st, f"acc{i}", [128, 512], F32, psum=True) for i in range(4)]
                tpk = Ring(P, st, "tpk", [128, 8, 128], BF16, 2, psum=True)

                P.dma("sp", lamt.d, [(lamt[:], D[f"lam{l}"])], writes=[lamt.r])
                P.dma("sp", g256.d, [(g256[:], D[f"subln{l}"].partition_broadcast(128))], writes=[g256.r])
                P.op("dve", lambda e: e.memset(ones[:], 1.0), writes=[ones.r])
                P.op("dve", lambda e: e.tensor_tensor(out=lam2[:], in0=lamt[:].rearrange("p (a b) -> p a b", b=2)[:, :, 0],
                                                      in1=lamt[:].rearrange("p (a b) -> p a b", b=2)[:, :, 1], op=ALU.mult),
                     reads=[lamt.r], writes=[lam2.r])
                acc0 = accr[0]
                P.op("pe", lambda e: e.matmul(acc0[:, 0:2], lhsT=ones[:], rhs=lam2[:], start=True, stop=True), reads=[ones.r, lam2.r], writes=[acc0.r])
                P.op("act", lambda e: e.activation(out=lam2[:], in_=acc0[:, 0:2], func=AF.Exp), reads=[acc0.r], writes=[lam2.r])
                P.op("dve", lambda e: e.tensor_tensor(out=nlam[:], in0=lam2[:, 1:2], in1=lam2[:, 0:1], op=ALU.subtract), reads=[lam2.r], writes=[nlam.r])
                P.op("dve", lambda e: e.tensor_scalar(out=nlam[:], in0=nlam[:], scalar1=-LAM_INIT[l], scalar2=None, op0=ALU.add), reads=[nlam.r], writes=[nlam.r])
                P.op("dve", lambda e: e.tensor_scalar(out=g256[:], in0=g256[:], scalar1=1.0 - LAM_INIT[l], scalar2=None, op0=ALU.mult), reads=[g256.r], writes=[g256.r])
                P.op("dve", lambda e: e.memset(vaug[:, :, 256:257], 1.0), writes=[vaug.r])
                P.op("dve", lambda e: e.memset(vaug[0:112, 0, 256:257], 0.0), writes=[vaug.r])
                P.op("dve", lambda e: e.memset(vac[:, :, 256:257], 1.0), writes=[vac.r])

                def attend(h, qap, nq, ktiles, zap, og_dst_rows):
                    sw = min(nq, 128)
                    nsub = nq // sw
                    for j in range(2):
                        for ki, kt in enumerate(ktiles):
                            s0 = kt["s0"]; nk = kt["nk"]
                            stp = str_.next()
                            P.op("pe", lambda e: e.matmul(stp[0:nk, s0 * sw:nq], lhsT=kt["k"](j), rhs=qap(j)[:, s0 * sw:nq], start=True, stop=True),
                                 reads=kt["res"] + qap.res, writes=[stp.r])
                            pt = ptr.next()
                            P.op("act", lambda e: e.activation(out=pt[0:nk, s0 * sw:nq], in_=stp[0:nk, s0 * sw:nq], func=AF.Exp, scale=scale),
                                 reads=[stp.r], writes=[pt.r])
                            if kt["diag"]:
                                P.op("pool", lambda e: e.tensor_tensor(out=pt[0:nk, s0 * sw:(s0 + 1) * sw], in0=pt[0:nk, s0 * sw:(s0 + 1) * sw],
                                                                       in1=dmask[0:nk, 0:sw], op=ALU.mult),
                                     reads=[pt.r, dmask.r], writes=[pt.r])
                            for s in range(s0, nsub):
                                last = (ki == len(ktiles) - 1) or (ktiles[ki + 1]["s0"] > s)
                                P.op("pe", lambda e, s=s: e.matmul(accr[s][0:sw, 0:257], lhsT=pt[0:nk, s * sw:(s + 1) * sw], rhs=kt["v"],
                                                                   start=(ki == 0), stop=last),
                                     reads=[pt.r] + kt["res"], writes=[accr[s].r], inc=last)
                        for s in range(nsub):
                            rc = rec.next(); a = accr[s]
                            P.op("dve", lambda e: e.tensor_scalar(out=rc[0:sw, :], in0=a[0:sw, 256:257], scalar1=1e-30, scalar2=None, op0=ALU.max),
                                 reads=[a.r], writes=[rc.r])
                            P.op("dve", lambda e: e.reciprocal(out=rc[0:sw, :], in_=rc[0:sw, :]), reads=[rc.r], writes=[rc.r])
                            if j == 0:
                                P.op("dve", lambda e: e.tensor_scalar(out=r1[0:sw, s, :], in0=a[0:sw, 0:256], scalar1=rc[0:sw, :], scalar2=None, op0=ALU.mult),
                                     reads=[a.r, rc.r], writes=[r1.r])
                            else:
                                P.op("dve", lambda e: e.tensor_tensor(out=rc[0:sw, :], in0=rc[0:sw, :], in1=nlam[0:sw, :], op=ALU.mult),
                                     reads=[rc.r, nlam.r], writes=[rc.r])
                                P.op("dve", lambda e: e.scalar_tensor_tensor(out=res[0:sw, s, :], in0=a[0:sw, 0:256], scalar=rc[0:sw, :], in1=r1[0:sw, s, :],
                                                                             op0=ALU.mult, op1=ALU.add),
                                     reads=[a.r, rc.r, r1.r], writes=[res.r])
                    for s in range(nsub):
                        ss = ssr.next(); g = gz.next(); ob = ogb.next()
                        P.op("act", lambda e: e.activation(out=junk[0:sw, :], in_=res[0:sw, s, :], func=AF.Square, accum_out=ss[0:sw, :]),
                             reads=[res.r], writes=[junk.r, ss.r])
                        P.op("dve", lambda e: e.tensor_scalar(out=ss[0:sw, :], in0=ss[0:sw, :], scalar1=1.0 / 256, scalar2=EPS, op0=ALU.mult, op1=ALU.add),
                             reads=[ss.r], writes=[ss.r])
                        P.op("act", lambda e: e.activation(out=ss[0:sw, :], in_=ss[0:sw, :], func=AF.Sqrt), reads=[ss.r], writes=[ss.r])
                        P.op("dve", lambda e: e.reciprocal(out=ss[0:sw, :], in_=ss[0:sw, :]), reads=[ss.r], writes=[ss.r])
                        P.op("pool", lambda e: e.tensor_tensor(out=g[0:sw, :], in0=zap(s), in1=g256[0:sw, :], op=ALU.mult),
                             reads=zap.res + [g256.r], writes=[g.r])
                        P.op("dve", lambda e: e.scalar_tensor_tensor(out=ob[0:sw, :], in0=res[0:sw, s, :], scalar=ss[0:sw, :], in1=g[0:sw, :],
                                                                     op0=ALU.mult, op1=ALU.mult),
                             reads=[res.r, ss.r, g.r], writes=[ob.r])
                        r0 = og_dst_rows(s)
                        P.dma("sp", ob.d, [(D["ogS"][r0:r0 + sw, h * 256:(h + 1) * 256], ob[0:sw, :])], reads=[ob.r], writes=[RD["ogS"]])

                class AP_:
                    def __init__(self, fn, res):
                        self.fn = fn; self.res = res

                    def __call__(self, *a):
                        return self.fn(*a)

                nqb = (NT + 3) // 4
                for h in range(8):
                    P.dma("sp", kts.d, [(kts[:, j, :], D["kT"][2 * h + j, :, 0:U]) for j in range(2)], reads=[RD["kT"]], writes=[kts.r])
                    P.dma("sp", vaug.d, [(vaug[:, :, 0:256], D["vS"][0:U, h * 256:(h + 1) * 256].rearrange("(t p) e -> p t e", p=128))],
                          reads=[RD["vS"]], writes=[vaug.r])
                    for qb in range(nqb):
                        t0 = qb * 4; nt = min(4, NT - t0); nq = nt * 128
                        qt = qtr.next(); zt = ztr.next()
                        P.dma("sp", qt.d, [(qt[:, j, 0:nq], D["qT"][2 * h + j, :, t0 * 128:t0 * 128 + nq]) for j in range(2)],
                              reads=[RD["qT"]], writes=[qt.r])
                        P.dma("sp", zt.d, [(zt[:, 0:nt, :], D["zS"][t0 * 128:t0 * 128 + nq, h * 256:(h + 1) * 256].rearrange("(t p) e -> p t e", p=128))],
                              reads=[RD["zS"]], writes=[zt.r])
                        ktl = []
                        for kt in range(t0 + nt):
                            ktl.append(dict(k=(lambda j, kt=kt: kts[:, j, kt * 128:(kt + 1) * 128]), v=vaug[:, kt, :], nk=128,
                                            s0=max(kt - t0, 0), diag=(kt >= t0), res=[kts.r, vaug.r]))
                        attend(h, AP_(lambda j, qt=qt: qt[:, j, :], [qt.r]), nq, ktl,
                               AP_(lambda s, zt=zt: zt[:, s, :], [zt.r]), lambda s, t0=t0: (t0 + s) * 128)
                    for b in range(2):
                        for which, dst in (("k", None), ("v", vac)):
                            cs_ = cst.next()
                            src = D[f"c{l}{which}"][b, :, h * 256:(h + 1) * 256].rearrange("(t p) e -> p t e", p=128)
                            P.dma("sp", cs_.d, [(cs_[:], src)], writes=[cs_.r])
                            if which == "k":
                                P.op("pool", lambda e: e.tensor_copy(out=cbf[:], in_=cs_[:]), reads=[cs_.r], writes=[cbf.r])
                                for j in range(2):
                                    for t8 in range(0, NPT, 8):
                                        n8 = min(8, NPT - t8)
                                        tp = tpk.next()
                                        for i in range(n8):
                                            P.op("pe", lambda e, i=i: e.transpose(out=tp[:, i, :], in_=cbf[:, t8 + i, j * 128:(j + 1) * 128], identity=ident[:]),
                                                 reads=[cbf.r, ident.r], writes=[tp.r], inc=(i == n8 - 1))
                                        P.op("act", lambda e: e.copy(out=ktc[:, j, t8 * 128:(t8 + n8) * 128], in_=tp[:, 0:n8, :]),
                                             reads=[tp.r], writes=[ktc.r])
                            else:
                                P.op("pool", lambda e: e.tensor_copy(out=vac[:, 0:NPT, 0:256], in_=cs_[:]), reads=[cs_.r], writes=[vac.r])
                        sr = NT * 128 + b * 64
                        P.dma("sp", ktc.d, [(ktc[:, j, PAST:PAST + 64], D["kT"][2 * h + j, :, sr:sr + 64]) for j in range(2)],
                              reads=[RD["kT"]], writes=[ktc.r])
                        P.dma("sp", vac.d, [(vac[0:64, NPT, 0:256], D["vS"][sr:sr + 64, h * 256:(h + 1) * 256])], reads=[RD["vS"]], writes=[vac.r])
                        P.dma("sp", qsm.d, [(qsm[:, j, :], D["qT"][2 * h + j, :, sr:sr + 64]) for j in range(2)], reads=[RD["qT"]], writes=[qsm.r])
                        P.dma("sp", zsm.d, [(zsm[:], D["zS"][sr:sr + 64, h * 256:(h + 1) * 256])], reads=[RD["zS"]], writes=[zsm.r])
                        ktl = []
                        for kt in range(NPT + 1):
                            nk = 128 if kt < NPT else 64
                            ktl.append(dict(k=(lambda j, kt=kt, nk=nk: ktc[:, j, kt * 128:kt * 128 + nk]), v=vac[0:nk, kt, :], nk=nk,
                                            s0=0, diag=False, res=[ktc.r, vac.r]))
                        attend(h, AP_(lambda j: qsm[:, j, :], [qsm.r]), 64, ktl, AP_(lambda s: zsm[:, :], [zsm.r]), lambda s, sr=sr: sr)
                P.barrier()

        class AP_:
            def __init__(self, fn, res):
                self.fn = fn; self.res = res

            def __call__(self, *a):
                return self.fn(*a)

        def attn_swa(l):
            scale = 64 ** -0.5
            with ExitStack() as st:
                kT2 = T(P, st, "kT2", [128, 4, U], BF16)
                vaug = T(P, st, "vaug", [128, NT, 4, 65], BF16)
                qtr = Ring(P, st, "qt", [128, 16, 128], BF16, 2)
                ztr = Ring(P, st, "zt", [128, 2048], BF16, 2)
                ogr = Ring(P, st, "ogt", [128, 2048], BF16, 2)
                ptr = Ring(P, st, "pt", [128, 4, 128], BF16, 4, dma=False)
                oacc = Ring(P, st, "oacc", [128, 4, 65], F32, 2, dma=False)
                den = Ring(P, st, "den", [128, 4], F32, 2, dma=False)
                otmp = Ring(P, st, "otmp", [128, 4, 64], F32, 2, dma=False)
                esink = T(P, st, "esink", [128, 32], F32)
                cst = T(P, st, "cst", [128, 256], F32)
                cbd = T(P, st, "cbd", [128, 512], BF16, dma=False)
                kTs = T(P, st, "kTs", [128, 4, 192], BF16)
                vas = T(P, st, "vas", [128, 2, 4, 65], BF16)
                vst = T(P, st, "vst", [128, 4, 64], F32)
                vnb = T(P, st, "vnb", [64, 4, 64], BF16)
                str_ = Ring(P, st, "st", [128, 4, 128], F32, 4, psum=True)
                accr = Ring(P, st, "acc", [128, 512], F32, 2, psum=True)
                tpk = Ring(P, st, "tpk", [128, 8, 128], BF16, 1, psum=True)

                P.dma("sp", esink.d, [(esink[:], D["sinks"].partition_broadcast(128))], writes=[esink.r])
                P.op("act", lambda e: e.activation(out=esink[:], in_=esink[:], func=AF.Exp), reads=[esink.r], writes=[esink.r])
                P.dma("sp", kT2.d, [(kT2[:], D["kT"][0:4, :, 0:U].rearrange("i p c -> p i c"))], reads=[RD["kT"]], writes=[kT2.r])
                P.op("dve", lambda e: e.memset(vaug[:].rearrange("p t g c -> p (t g) c")[:, :, 64:65], 1.0), writes=[vaug.r])
                P.op("dve", lambda e: e.memset(vaug[0:112, 0, :, 64:65], 0.0), writes=[vaug.r])
                P.dma("sp", vaug.d, [(vaug[:, t, :, 0:64], D["vS"][t * 128:(t + 1) * 128, 0:256].rearrange("p (g d) -> p g d", d=64)) for t in range(NT)],
                      reads=[RD["vS"]], writes=[vaug.r])

                def v3(tl, nq):
                    flat = tl[:].rearrange("p i c -> p (i c)")[:, 0:4 * nq]
                    return flat, flat.rearrange("p (i c) -> p i c", c=nq)

                def unit_tile(qt, nq, ktiles, zt, ogt, qcompact=False):
                    q3 = qt[:].rearrange("p i c -> p (i c)")[:, 0:16 * nq].rearrange("p (i c) -> p i c", c=nq) if qcompact else qt[:]
                    for g in range(4):
                        for hf in range(2):
                            h0 = 8 * g + 4 * hf
                            pts = []
                            for (kfn, vfn, nk, mk, kres) in ktiles:
                                stAB = [str_.next(), str_.next()]
                                sAB = [v3(x_, nq) for x_ in stAB]
                                for i in range(4):
                                    h = h0 + i; pr = h // 2; half = h % 2
                                    P.op("pe", lambda e, i=i, pr=pr, half=half: e.matmul(sAB[half][1][0:nk, i // 2, :], lhsT=kfn(g, half),
                                                                                         rhs=q3[64 * half:64 * half + 64, pr, :], start=True, stop=True),
                                         reads=kres + [qt.r], writes=[stAB[half].r], inc=(i >= 2))
                                pt = ptr.next()
                                pf, p3 = v3(pt, nq)
                                for half in range(2):
                                    P.op("act", lambda e, half=half: e.activation(out=pf[0:nk, half * 2 * nq:(half + 1) * 2 * nq], in_=sAB[half][0][0:nk, 0:2 * nq], func=AF.Exp, scale=scale),
                                         reads=[stAB[half].r], writes=[pt.r])
                                if mk is not None:
                                    P.op("dve", lambda e: e.tensor_tensor(out=p3[0:nk], in0=p3[0:nk],
                                                                          in1=mk.unsqueeze(1).broadcast_to([nk, 4, nq]), op=ALU.mult),
                                         reads=[pt.r, swm.r], writes=[pt.r])
                                pts.append(p3)
                                pts[-1] = (p3, pt)
                            acc = accr.next()
                            a3 = acc[:, 0:260].rearrange("p (i c) -> p i c", c=65)
                            for i in range(4):
                                for ki, (kfn, vfn, nk, mk, kres) in enumerate(ktiles):
                                    last = ki == len(ktiles) - 1
                                    P.op("pe", lambda e, i=i, ki=ki: e.matmul(a3[0:nq, i, :], lhsT=pts[ki][0][0:nk, (i % 2) * 2 + i // 2, :], rhs=vfn(g),
                                                                              start=(ki == 0), stop=last),
                                         reads=[pts[ki][1].r] + kres, writes=[acc.r], inc=(last and i == 3))
                            oa = oacc.next(); dn = den.next()
                            P.op("act", lambda e: e.copy(out=oa[0:nq].rearrange("p i c -> p (i c)"), in_=acc[0:nq, 0:260]), reads=[acc.r], writes=[oa.r])
                            P.op("dve", lambda e: e.tensor_tensor(out=dn[0:nq, :], in0=oa[0:nq, :, 64], in1=esink[0:nq, h0:h0 + 4], op=ALU.add),
                                 reads=[oa.r, esink.r], writes=[dn.r])
                            P.op("dve", lambda e: e.reciprocal(out=dn[0:nq, :], in_=dn[0:nq, :]), reads=[dn.r], writes=[dn.r])
                            for i in range(4):
                                h = h0 + i
                                P.op("dve", lambda e, i=i, h=h: e.scalar_tensor_tensor(out=ogt[0:nq, h * 64:(h + 1) * 64], in0=oa[0:nq, i, 0:64], scalar=dn[0:nq, i:i + 1],
                                                                                       in1=zt[0:nq, h * 64:(h + 1) * 64], op0=ALU.mult, op1=ALU.mult),
                                     reads=[oa.r, dn.r, zt.r], writes=[ogt.r])

                for m in range(NT):
                    qt = qtr.next(); zt = ztr.next(); ogt = ogr.next()
                    P.dma("sp", qt.d, [(qt[:], D["qT"][0:16, :, tile_rows(m)].rearrange("i p c -> p i c"))], reads=[RD["qT"]], writes=[qt.r])
                    P.dma("sp", zt.d, [(zt[:], D["zS"][tile_rows(m), :])], reads=[RD["zS"]], writes=[zt.r])
                    ktl = []
                    for j, kt in enumerate((m - 1, m)):
                        if kt < 0:
                            continue
                        ktl.append(((lambda g, half, kt=kt: kT2[64 * half:64 * half + 64, g, kt * 128:(kt + 1) * 128]),
                                    (lambda g, kt=kt: vaug[:, kt, g, :]), 128, swm[:, j, :], [kT2.r, vaug.r]))
                    unit_tile(qt, 128, ktl, zt, ogt)
                    P.dma("sp", ogt.d, [(D["ogS"][tile_rows(m), :], ogt[:])], reads=[ogt.r], writes=[RD["ogS"]])
                P.op("dve", lambda e: e.memset(vas[:].rearrange("p t g c -> p (t g) c")[:, :, 64:65], 1.0), writes=[vas.r])
                for b in range(2):
                    sr = NT * 128 + b * 64
                    P.dma("sp", cst.d, [(cst[:], D["c1k"][b])], writes=[cst.r])
                    cd = cbd[:].rearrange("p (g two d) -> p g two d", two=2, d=64)
                    for j in range(2):
                        P.op("pool", lambda e, j=j: e.tensor_copy(out=cd[:, :, j, :], in_=cst[:].rearrange("p (g d) -> p g d", d=64)), reads=[cst.r], writes=[cbd.r])
                    tp = tpk.next()
                    for g in range(4):
                        P.op("pe", lambda e, g=g: e.transpose(out=tp[:, g, :], in_=cbd[:, g * 128:(g + 1) * 128], identity=ident[:]),
                             reads=[cbd.r, ident.r], writes=[tp.r], inc=(g == 3))
                    P.op("act", lambda e: e.copy(out=kTs[:, :, 0:128], in_=tp[:, 0:4, :]), reads=[tp.r], writes=[kTs.r])
                    P.dma("sp", kTs.d, [(kTs[:, :, 128:192], D["kT"][0:4, :, sr:sr + 64].rearrange("i p c -> p i c"))], reads=[RD["kT"]], writes=[kTs.r])
                    P.dma("sp", vst.d, [(vst[:], D["c1v"][b].rearrange("p (g d) -> p g d", d=64))], writes=[vst.r])
                    P.op("pool", lambda e: e.tensor_copy(out=vas[:, 0, :, 0:64], in_=vst[:]), reads=[vst.r], writes=[vas.r])
                    P.dma("sp", vnb.d, [(vnb[:], D["vS"][sr:sr + 64, 0:256].rearrange("p (g d) -> p g d", d=64))], reads=[RD["vS"]], writes=[vnb.r])
                    P.op("pool", lambda e: e.tensor_copy(out=vas[0:64, 1, :, 0:64], in_=vnb[:]), reads=[vnb.r], writes=[vas.r])
                    qt = qtr.next(); zt = ztr.next(); ogt = ogr.next()
                    P.dma("sp", qt.d, [(qt[:].rearrange("p i c -> p (i c)")[:, 0:1024].rearrange("p (i c) -> p i c", c=64), D["qT"][0:16, :, sr:sr + 64].rearrange("i p c -> p i c"))], reads=[RD["qT"]], writes=[qt.r])
                    P.dma("sp", zt.d, [(zt[0:64, :], D["zS"][sr:sr + 64, :])], reads=[RD["zS"]], writes=[zt.r])
                    ktl = [((lambda g, half: kTs[64 * half:64 * half + 64, g, 0:128]), (lambda g: vas[:, 0, g, :]), 128, None, [kTs.r, vas.r]),
                           ((lambda g, half: kTs[64 * half:64 * half + 64, g, 128:192]), (lambda g: vas[0:64, 1, g, :]), 64, None, [kTs.r, vas.r])]
                    unit_tile(qt, 64, ktl, zt, ogt, qcompact=True)
                    P.dma("sp", ogt.d, [(D["ogS"][sr:sr + 64, :], ogt[0:64, :])], reads=[ogt.r], writes=[RD["ogS"]])
                P.barrier()

        def attn_dsa(l):
            scale = 128 ** -0.5
            NIT = 14
            with ExitStack() as st:
                WMAX = max(U, PAST + 64)
                kiT2 = T(P, st, "kiT2", [128, WMAX], BF16)
                sc = T(P, st, "sc", [128, WMAX], F32, dma=False)
                junkb = T(P, st, "junkb", [128, WMAX], BF16, dma=False)
                mq = T(P, st, "mq", [128, WMAX], BF16, dma=False)
                qir = Ring(P, st, "qi", [128, 8, 128], BF16, 2)
                wir = Ring(P, st, "wi", [128, 16], F32, 2)
                Dg = T(P, st, "Dg", [128, 16, 128], BF16, dma=False)
                rhr = Ring(P, st, "rh", [128, 512], BF16, 4, dma=False)
                identf = T(P, st, "identf", [128, 128], F32, dma=False)
                sm = {k: T(P, st, "b_" + k, [128, 1], F32, dma=False) for k in ("lo", "hi", "mid", "cnt", "pred", "d1", "d2")}
                mstg = Ring(P, st, "mstg", [128, 8, 128], BF16, 2)
                cis = T(P, st, "cis", [128, NPT, 64], F32)
                cib = T(P, st, "cib", [128, NPT, 128], BF16, dma=False)
                yr = Ring(P, st, "y", [128, 512], F32, 4, psum=True)
                scp = Ring(P, st, "scp", [128, 512], F32, 2, psum=True)
                tpm = Ring(P, st, "tpm", [128, 8, 128], BF16, 2, psum=True)
                P.op("dve", lambda e: e.tensor_copy(out=identf[:], in_=ident[:]), reads=[ident.r], writes=[identf.r])

                def index_tile(qi, wi, nq, nkeys, topk, pad_cols, diag, mask_dst):
                    for h in range(16):
                        P.op("dve", lambda e, h=h: e.tensor_scalar(out=Dg[0:nq, h, 0:nq], in0=identf[0:nq, 0:nq], scalar1=wi[0:nq, h:h + 1], scalar2=None, op0=ALU.mult),
                             reads=[identf.r, wi.r], writes=[Dg.r])
                    nkb = (nkeys + 511) // 512
                    for kb in range(nkb):
                        k0 = kb * 512; nk = min(512, nkeys - k0)
                        sp_ = scp.next()
                        rhs_ = []

                        def dg(h):
                            P.op("pe", lambda e: e.matmul(sp_[0:nq, 0:nk], lhsT=Dg[0:nq, h, 0:nq], rhs=rhs_[h][0:nq, 0:nk], start=(h == 0), stop=(h == 15)),
                                 reads=[Dg.r, rhs_[h].r], writes=[sp_.r], inc=(h == 15))
                        for h in range(16):
                            pr = h // 2; half = h % 2
                            y = yr.next()
                            P.op("pe", lambda e: e.matmul(y[0:nq, 0:nk], lhsT=qi[64 * half:64 * half + 64, pr, 0:nq], rhs=kiT2[64 * half:64 * half + 64, k0:k0 + nk],
                                                          start=True, stop=True), reads=[qi.r, kiT2.r], writes=[y.r])
                            rh = rhr.next()
                            if h % 2 == 0:
                                P.op("act", lambda e: e.activation(out=rh[0:nq, 0:nk], in_=y[0:nq, 0:nk], func=AF.Relu), reads=[y.r], writes=[rh.r])
                            else:
                                P.op("dve", lambda e: e.tensor_scalar(out=rh[0:nq, 0:nk], in0=y[0:nq, 0:nk], scalar1=0.0, scalar2=None, op0=ALU.max), reads=[y.r], writes=[rh.r])
                            rhs_.append(rh)
                            if h >= 2:
                                dg(h - 2)
                        dg(14); dg(15)
                        P.op("act", lambda e: e.copy(out=sc[0:nq, k0:k0 + nk], in_=sp_[0:nq, 0:nk]), reads=[sp_.r], writes=[sc.r])
                    lo, hi, mid, cnt, pred, d1, d2 = (sm[k] for k in ("lo", "hi", "mid", "cnt", "pred", "d1", "d2"))
                    P.op("dve", lambda e: e.tensor_reduce(out=hi[0:nq, :], in_=sc[0:nq, 0:nkeys], axis=mybir.AxisListType.X, op=ALU.max), reads=[sc.r], writes=[hi.r])
                    P.op("dve", lambda e: e.tensor_reduce(out=lo[0:nq, :], in_=sc[0:nq, 0:nkeys], axis=mybir.AxisListType.X, op=ALU.min), reads=[sc.r], writes=[lo.r])
                    P.op("dve", lambda e: e.tensor_scalar(out=lo[0:nq, :], in0=lo[0:nq, :], scalar1=-1.0, scalar2=None, op0=ALU.add), reads=[lo.r], writes=[lo.r])
                    if pad_cols:
                        P.op("dve", lambda e: e.memset(sc[0:nq, 0:pad_cols], NEGBIG), writes=[sc.r])
                    if diag:
                        P.op("dve", lambda e: e.memset(sc[0:64, nkeys - 64:nkeys], NEGBIG), writes=[sc.r])
                    for it in range(NIT):
                        P.op("dve", lambda e: e.tensor_tensor(out=mid[0:nq, :], in0=lo[0:nq, :], in1=hi[0:nq, :], op=ALU.add), reads=[lo.r, hi.r], writes=[mid.r])
                        P.op("dve", lambda e: e.tensor_scalar(out=mid[0:nq, :], in0=mid[0:nq, :], scalar1=0.5, scalar2=None, op0=ALU.mult), reads=[mid.r], writes=[mid.r])
                        P.op("dve", lambda e: e.tensor_scalar(out=junkb[0:nq, 0:nkeys], in0=sc[0:nq, 0:nkeys], scalar1=mid[0:nq, :], scalar2=None,
                                                              op0=ALU.is_ge, op1=ALU.add, accum_out=cnt[0:nq, :]),
                             reads=[sc.r, mid.r], writes=[junkb.r, cnt.r])
                        P.op("dve", lambda e: e.tensor_scalar(out=pred[0:nq, :], in0=cnt[0:nq, :], scalar1=float(topk) - 0.5, scalar2=None, op0=ALU.is_ge), reads=[cnt.r], writes=[pred.r])
                        P.op("dve", lambda e: e.tensor_tensor(out=d1[0:nq, :], in0=mid[0:nq, :], in1=lo[0:nq, :], op=ALU.subtract), reads=[mid.r, lo.r], writes=[d1.r])
                        P.op("dve", lambda e: e.tensor_tensor(out=d2[0:nq, :], in0=hi[0:nq, :], in1=mid[0:nq, :], op=ALU.subtract), reads=[mid.r, hi.r], writes=[d2.r])
                        P.op("dve", lambda e: e.scalar_tensor_tensor(out=lo[0:nq, :], in0=d1[0:nq, :], scalar=pred[0:nq, :], in1=lo[0:nq, :], op0=ALU.mult, op1=ALU.add),
                             reads=[d1.r, pred.r, lo.r], writes=[lo.r])
                        P.op("dve", lambda e: e.scalar_tensor_tensor(out=hi[0:nq, :], in0=d2[0:nq, :], scalar=pred[0:nq, :], in1=mid[0:nq, :], op0=ALU.mult, op1=ALU.add),
                             reads=[d2.r, pred.r, mid.r], writes=[hi.r])
                    P.op("dve", lambda e: e.tensor_scalar(out=mq[0:nq, 0:nkeys], in0=sc[0:nq, 0:nkeys], scalar1=lo[0:nq, :], scalar2=None, op0=ALU.is_ge),
                         reads=[sc.r, lo.r], writes=[mq.r])
                    nkt = (nkeys + 127) // 128
                    for t8 in range(0, nkt, 8):
                        n8 = min(8, nkt - t8)
                        tp = tpm.next()
                        for i in range(n8):
                            kk = min(128, nkeys - (t8 + i) * 128)
                            P.op("pe", lambda e, i=i, kk=kk: e.transpose(out=tp[0:kk, i, 0:nq], in_=mq[0:nq, (t8 + i) * 128:(t8 + i) * 128 + kk], identity=ident[0:nq, 0:nq]),
                                 reads=[mq.r, ident.r], writes=[tp.r], inc=(i == n8 - 1))
                        sg = mstg.next()
                        P.op("act", lambda e: e.copy(out=sg[:, 0:n8, 0:nq], in_=tp[:, 0:n8, 0:nq]), reads=[tp.r], writes=[sg.r])
                        mask_dst(t8, n8, sg)

                P.dma("sp", kiT2.d, [(kiT2[:, 0:U], D["kiT"][0, :, 0:U])], reads=[RD["kiT"]], writes=[kiT2.r])
                for m in range(NT):
                    qi = qir.next(); wi = wir.next()
                    P.dma("sp", qi.d, [(qi[:], D["qiT"][0:8, :, tile_rows(m)].rearrange("i p c -> p i c"))], reads=[RD["qiT"]], writes=[qi.r])
                    P.dma("sp", wi.d, [(wi[:], D["wiS"][tile_rows(m), :])], reads=[RD["wiS"]], writes=[wi.r])

                    def mdst(t8, n8, sg, m=m):
                        P.dma("sp", sg.d, [(D["mkT"][m, :, t8:t8 + n8, :], sg[:, 0:n8, :])], reads=[sg.r], writes=[RD["mkT"]])
                    index_tile(qi, wi, 128, (m + 1) * 128, cfg.topk_p, 112, True, mdst)
                for b in range(2):
                    sr = NT * 128 + b * 64
                    P.dma("sp", cis.d, [(cis[:], D["c2i"][b].rearrange("(t p) d -> p t d", p=128))], writes=[cis.r])
                    for j in range(2):
                        P.op("pool", lambda e, j=j: e.tensor_copy(out=cib[:, :, j * 64:(j + 1) * 64], in_=cis[:]), reads=[cis.r], writes=[cib.r])
                    for t8 in range(0, NPT, 8):
                        n8 = min(8, NPT - t8)
                        tp = tpm.next()
                        for i in range(n8):
                            P.op("pe", lambda e, i=i: e.transpose(out=tp[:, i, :], in_=cib[:, t8 + i, :], identity=ident[:]), reads=[cib.r, ident.r], writes=[tp.r], inc=(i == n8 - 1))
                        P.op("act", lambda e: e.copy(out=kiT2[:, t8 * 128:(t8 + n8) * 128], in_=tp[:, 0:n8, :]), reads=[tp.r], writes=[kiT2.r])
                    P.dma("sp", kiT2.d, [(kiT2[:, PAST:PAST + 64], D["kiT"][0, :, sr:sr + 64])], reads=[RD["kiT"]], writes=[kiT2.r])
                    qi = qir.next(); wi = wir.next()
                    P.dma("sp", qi.d, [(qi[:, :, 0:64], D["qiT"][0:8, :, sr:sr + 64].rearrange("i p c -> p i c"))], reads=[RD["qiT"]], writes=[qi.r])
                    P.dma("sp", wi.d, [(wi[0:64, :], D["wiS"][sr:sr + 64, :])], reads=[RD["wiS"]], writes=[wi.r])

                    def mdst_s(t8, n8, sg, b=b):
                        P.dma("sp", sg.d, [(D["mkS"][b, :, t8:t8 + n8, :], sg[:, 0:n8, 0:64])], reads=[sg.r], writes=[RD["mkS"]])
                    index_tile(qi, wi, 64, PAST + 64, cfg.topk_s, 0, False, mdst_s)
                P.barrier()
            with ExitStack() as st:
                WMAX = max(U, PAST + 64)
                NKT = max(NT, NPT + 1)
                kTg = T(P, st, "kTg", [128, WMAX], BF16)
                vag = T(P, st, "vag", [128, NKT, 129], BF16)
                qtr = Ring(P, st, "qt", [128, 4, 128], BF16, 2)
                mkr = Ring(P, st, "mk", [128, NKT, 128], BF16, 2)
                ztr = Ring(P, st, "zt", [128, 512], BF16, 2)
                ogr = Ring(P, st, "ogt", [128, 512], BF16, 2)
                ptr = Ring(P, st, "pt", [128, 4, 128], BF16, 3, dma=False)
                rec = Ring(P, st, "rec", [128, 1], F32, 4, dma=False)
                cst = T(P, st, "cst", [128, NPT, 128], F32)
                cbf = T(P, st, "cbf", [128, NPT, 128], BF16, dma=False)
                str_ = Ring(P, st, "st", [128, 4, 128], F32, 2, psum=True)
                accr = [T(P, st, f"acc{i}", [128, 512], F32, psum=True) for i in range(4)]
                tpk = Ring(P, st, "tpk", [128, 8, 128], BF16, 2, psum=True)

                def v3(tl, nq):
                    flat = tl[:].rearrange("p i c -> p (i c)")[:, 0:4 * nq]
                    return flat, flat.rearrange("p (i c) -> p i c", c=nq)

                def att_tile(g, qt, nq, nkt, nk_last, mk, zt, ogt):
                    qf, q3 = v3(qt, nq)
                    for kt in range(nkt):
                        nk = 128 if kt < nkt - 1 else nk_last
                        stp = str_.next(); pt = ptr.next()
                        sf, s3 = v3(stp, nq); pf, p3 = v3(pt, nq)
                        P.op("pe", lambda e: e.matmul(sf[0:nk, :], lhsT=kTg[:, kt * 128:kt * 128 + nk], rhs=qf, start=True, stop=True),
                             reads=[kTg.r, qt.r], writes=[stp.r])
                        P.op("act", lambda e: e.activation(out=pf[0:nk, :], in_=sf[0:nk, :], func=AF.Exp, scale=scale), reads=[stp.r], writes=[pt.r])
                        P.op("dve", lambda e: e.tensor_tensor(out=p3[0:nk], in0=p3[0:nk],
                                                              in1=mk[0:nk, kt, 0:nq].unsqueeze(1).broadcast_to([nk, 4, nq]), op=ALU.mult),
                             reads=[pt.r, mk.r], writes=[pt.r])
                        for i in range(4):
                            P.op("pe", lambda e, i=i: e.matmul(accr[i][0:nq, 0:129], lhsT=p3[0:nk, i, :], rhs=vag[0:nk, kt, :], start=(kt == 0), stop=(kt == nkt - 1)),
                                 reads=[pt.r, vag.r], writes=[accr[i].r], inc=(kt == nkt - 1))
                    for i in range(4):
                        rc = rec.next(); a = accr[i]
                        P.op("dve", lambda e: e.tensor_scalar(out=rc[0:nq, :], in0=a[0:nq, 128:129], scalar1=1e-30, scalar2=None, op0=ALU.max), reads=[a.r], writes=[rc.r])
                        P.op("dve", lambda e: e.reciprocal(out=rc[0:nq, :], in_=rc[0:nq, :]), reads=[rc.r], writes=[rc.r])
                        P.op("dve", lambda e: e.scalar_tensor_tensor(out=ogt[0:nq, i * 128:(i + 1) * 128], in0=a[0:nq, 0:128], scalar=rc[0:nq, :],
                                                                     in1=zt[0:nq, i * 128:(i + 1) * 128], op0=ALU.mult, op1=ALU.mult),
                             reads=[a.r, rc.r, zt.r], writes=[ogt.r])

                P.op("dve", lambda e: e.memset(vag[:, :, 128:129], 1.0), writes=[vag.r])
                for g in range(4):
                    P.dma("sp", kTg.d, [(kTg[:, 0:U], D["kT"][g, :, 0:U])], reads=[RD["kT"]], writes=[kTg.r])
                    P.dma("sp", vag.d, [(vag[:, 0:NT, 0:128], D["vS"][0:U, g * 128:(g + 1) * 128].rearrange("(t p) e -> p t e", p=128))], reads=[RD["vS"]], writes=[vag.r])
                    P.op("dve", lambda e: e.memset(vag[:, :, 128:129], 1.0), writes=[vag.r])
                    P.op("dve", lambda e: e.memset(vag[0:112, 0, 128:129], 0.0), writes=[vag.r])
                    for m in range(NT):
                        qt = qtr.next(); mk = mkr.next(); zt = ztr.next(); ogt = ogr.next()
                        P.dma("sp", qt.d, [(qt[:], D["qT"][4 * g:4 * g + 4, :, tile_rows(m)].rearrange("i p c -> p i c"))], reads=[RD["qT"]], writes=[qt.r])
                        P.dma("sp", mk.d, [(mk[:, 0:m + 1, :], D["mkT"][m, :, 0:m + 1, :])], reads=[RD["mkT"]], writes=[mk.r])
                        P.dma("sp", zt.d, [(zt[:], D["zS"][tile_rows(m), g * 512:(g + 1) * 512])], reads=[RD["zS"]], writes=[zt.r])
                        att_tile(g, qt, 128, m + 1, 128, mk, zt, ogt)
                        P.dma("sp", ogt.d, [(D["ogS"][tile_rows(m), g * 512:(g + 1) * 512], ogt[:])], reads=[ogt.r], writes=[RD["ogS"]])
                    for b in range(2):
                        sr = NT * 128 + b * 64
                        P.dma("sp", cst.d, [(cst[:], D["c2k"][b, :, g * 128:(g + 1) * 128].rearrange("(t p) d -> p t d", p=128))], writes=[cst.r])
                        P.op("pool", lambda e: e.tensor_copy(out=cbf[:], in_=cst[:]), reads=[cst.r], writes=[cbf.r])
                        for t8 in range(0, NPT, 8):
                            n8 = min(8, NPT - t8)
                            tp = tpk.next()
                            for i in range(n8):
                                P.op("pe", lambda e, i=i: e.transpose(out=tp[:, i, :], in_=cbf[:, t8 + i, :], identity=ident[:]), reads=[cbf.r, ident.r], writes=[tp.r], inc=(i == n8 - 1))
                            P.op("act", lambda e: e.copy(out=kTg[:, t8 * 128:(t8 + n8) * 128], in_=tp[:, 0:n8, :]), reads=[tp.r], writes=[kTg.r])
                        P.dma("sp", kTg.d, [(kTg[:, PAST:PAST + 64], D["kT"][g, :, sr:sr + 64])], reads=[RD["kT"]], writes=[kTg.r])
                        P.dma("sp", cst.d, [(cst[:], D["c2v"][b, :, g * 128:(g + 1) * 128].rearrange("(t p) d -> p t d", p=128))], reads=[cbf.r], writes=[cst.r])
                        P.op("pool", lambda e: e.tensor_copy(out=vag[:, 0:NPT, 0:128], in_=cst[:]), reads=[cst.r], writes=[vag.r])
                        P.dma("sp", vag.d, [(vag[0:64, NPT, 0:128], D["vS"][sr:sr + 64, g * 128:(g + 1) * 128])], reads=[RD["vS"]], writes=[vag.r])
                        P.op("dve", lambda e: e.memset(vag[:, :, 128:129], 1.0), writes=[vag.r])
                        qt = qtr.next(); mk = mkr.next(); zt = ztr.next(); ogt = ogr.next()
                        P.dma("sp", qt.d, [(v3(qt, 64)[1], D["qT"][4 * g:4 * g + 4, :, sr:sr + 64].rearrange("i p c -> p i c"))], reads=[RD["qT"]], writes=[qt.r])
                        P.dma("sp", mk.d, [(mk[:, 0:NPT + 1, 0:64], D["mkS"][b])], reads=[RD["mkS"]], writes=[mk.r])
                        P.dma("sp", zt.d, [(zt[0:64, :], D["zS"][sr:sr + 64, g * 512:(g + 1) * 512])], reads=[RD["zS"]], writes=[zt.r])
                        att_tile(g, qt, 64, NPT + 1, 64, mk, zt, ogt)
                        P.dma("sp", ogt.d, [(D["ogS"][sr:sr + 64, g * 512:(g + 1) * 512], ogt[0:64, :])], reads=[ogt.r], writes=[RD["ogS"]])
                P.barrier()

        nl = cfg.nlayers
        import os
        KSTOP = int(os.environ.get("KSTOP", "99"))
        for l in range(nl):
            kind = LAYER_KIND[l]
            KB = int(os.environ.get("KB", "99"))
            inproj(l)
            if KSTOP == 0 or (l == 1 and KB == 0):
                break
            if kind == "A":
                attn_diff(l)
            elif kind == "B":
                attn_swa(l)
            else:
                attn_dsa(l)
            if KSTOP == 1 or (l == 1 and KB == 1):
                break
            outproj(l, last=(l == nl - 1))
        P.barrier()
        print("program built: ninst", P.ninst, "nwait", P.nwait, "nsem", P.nsem)
    return nc


def rope_tables(cfg):
    UT, NT, PAST = cfg.UT, cfg.NT, cfg.PAST
    pos = np.zeros((UT,), np.float32)
    u = np.arange(cfg.U)
    pos[:cfg.U] = np.maximum(u - 112, 0)
    pos[cfg.U:cfg.U + 64] = PAST + np.arange(64)
    pos[cfg.U + 64:] = PAST + np.arange(64)
    out = []
    for rd in (32, 16):
        half = rd // 2
        inv = (np.float32(500000.0) ** (-(np.arange(half, dtype=np.float32) * np.float32(2.0) / np.float32(rd)))).astype(np.float32)
        ang = pos[:, None].astype(np.float32) * inv[None, :]
        out.append(np.concatenate([np.cos(ang), np.sin(ang)], axis=1).astype(np.float32))
    return out


def const_masks():
    r = np.arange(128)[:, None]
    c = np.arange(128)[None, :]
    dm = ((r < 64) | (c >= 64)).astype(np.float32)
    sw0 = (~((r < 64) & (c >= 64))).astype(np.float32)
    sw1 = dm
    bf = ml_dtypes.bfloat16
    return dm.astype(bf), np.stack([sw0, sw1]).astype(bf)


_CACHE = {}


def kernel(**inp):
    SEQ = inp["x_prompt"].shape[1]
    PAST = inp["cache_l0_k"].shape[1]
    NB = inp["x_prompt"].shape[0]
    cfg = Cfg(SEQ, PAST, nlayers=inp.pop("_nlayers", 4))
    key = (SEQ, PAST, cfg.nlayers)
    if key not in _CACHE:
        _CACHE[key] = build(cfg)
    nc = _CACHE[key]
    f32 = np.float32
    cs128, cs64 = rope_tables(cfg)
    dm, swm = const_masks()
    ident = np.eye(128, dtype=np.float32).astype(ml_dtypes.bfloat16)
    c = lambda a: np.ascontiguousarray(np.asarray(a, dtype=f32))
    common = dict(meta=c(inp["meta_tokens"]), sinks=c(inp["l1_sinks"]), fnorm=c(inp["final_norm"]),
                  ident=ident, cs128=cs128, cs64=cs64, dmask=dm, swm=swm)
    for l in range(4):
        common[f"normv{l}"] = c(inp[f"l{l}_norm"]); common[f"win{l}"] = c(inp[f"l{l}_w_in"]); common[f"wout{l}"] = c(inp[f"l{l}_w_out"])
    for l in (0, 3):
        common[f"lam{l}"] = c(np.stack([c(inp[f"l{l}_lam_q1"]), c(inp[f"l{l}_lam_k1"]), c(inp[f"l{l}_lam_q2"]), c(inp[f"l{l}_lam_k2"])]).T)
        common[f"subln{l}"] = c(inp[f"l{l}_subln"])
    in_maps = []
    for core in range(8):
        b = core % NB
        sb = slice(2 * core, 2 * core + 2)
        m = dict(common)
        m["xp"] = c(inp["x_prompt"][b]); m["xs"] = c(inp["x_sample"][sb]).reshape(128, 2048)
        for l in (0, 3):
            m[f"c{l}k"] = c(inp[f"cache_l{l}_k"][sb]).reshape(2, PAST, 2048)
            m[f"c{l}v"] = c(inp[f"cache_l{l}_v"][sb]).reshape(2, PAST, 2048)
        m["c1k"] = c(inp["cache_l1_k"][sb]).reshape(2, 128, 256); m["c1v"] = c(inp["cache_l1_v"][sb]).reshape(2, 128, 256)
        m["c2k"] = c(inp["cache_l2_k"][sb]).reshape(2, PAST, 512); m["c2v"] = c(inp["cache_l2_v"][sb]).reshape(2, PAST, 512)
        m["c2i"] = c(inp["cache_l2_kidx"][sb]).reshape(2, PAST, 64)
        in_maps.append(m)
    res = run_bass_kernel_spmd(nc, in_maps, core_ids=list(range(8))).results
    Tn = cfg.T
    P_ = lambda name, shp: np.stack([res[b][name] for b in range(NB)]).reshape((NB,) + shp)
    S_ = lambda name, shp: np.concatenate([res[cidx][name].reshape((2,) + shp) for cidx in range(8)], axis=0)
    outs = [P_("yp", (SEQ, 2048)), S_("ys", (64, 2048)),
            P_("p0k", (Tn, 8, 256)), P_("p0v", (Tn, 8, 256)), S_("s0k", (64, 8, 256)), S_("s0v", (64, 8, 256)),
            P_("p1k", (128, 4, 64)), P_("p1v", (128, 4, 64)), S_("s1k", (128, 4, 64)), S_("s1v", (128, 4, 64)),
            P_("p2k", (Tn, 4, 128)), P_("p2v", (Tn, 4, 128)), P_("p2i", (Tn, 64)),
            S_("s2k", (64, 4, 128)), S_("s2v", (64, 4, 128)), S_("s2i", (64, 64)),
            P_("p3k", (Tn, 8, 256)), P_("p3v", (Tn, 8, 256)), S_("s3k", (64, 8, 256)), S_("s3v", (64, 8, 256))]
    return tuple(np.ascontiguousarray(o.astype(np.float32)) for o in outs)
```

```python
import numpy as np
import ml_dtypes
from contextlib import ExitStack
import concourse.bass as bass
import concourse.mybir as mybir
from concourse.bass_utils import run_bass_kernel_spmd
from concourse.alu_op_type import AluOpType as ALU

AF = mybir.ActivationFunctionType
F32 = mybir.dt.float32
BF16 = mybir.dt.bfloat16
NEGBIG = -3.0e38
EPS = 1e-6


class Sem:
    def __init__(self, h, i):
        self.h = h
        self.i = i
        self.total = 0


class Res:
    __slots__ = ("name", "w", "r")

    def __init__(self, name):
        self.name = name
        self.w = None
        self.r = {}


class Prog:
    def __init__(self, nc, stack):
        self.nc = nc
        self.stack = stack
        self.eng = dict(pe=nc.tensor, act=nc.scalar, dve=nc.vector, pool=nc.gpsimd, sp=nc.sync)
        self.nsem = 0
        self.dsems = []
        self.esem = {k: self._new_sem("e_" + k) for k in self.eng}
        self.ecnt = {k: 0 for k in self.eng}
        self.known = {k: {} for k in self.eng}
        self.nwait = 0
        self.ninst = 0

    def _new_sem(self, name):
        h = self.stack.enter_context(self.nc.semaphore(name))
        self.nsem += 1
        return Sem(h, self.nsem)

    def new_dsem(self, name):
        if getattr(self, "free_dsems", None):
            return self.free_dsems.pop()
        s = self._new_sem(name)
        self.dsems.append(s)
        return s

    def release_dsem(self, s):
        if not hasattr(self, "free_dsems"):
            self.free_dsems = []
        self.free_dsems.append(s)

    def _wait(self, e, ev):
        if ev is None:
            return
        sem, val = ev
        if val <= 0:
            return
        if e == "pe" and sem is self.esem["pe"]:
            return
        kn = self.known[e]
        if kn.get(sem.i, 0) >= val:
            return
        self.eng[e].wait_ge(sem.h, val)
        kn[sem.i] = val
        self.nwait += 1

    def _deps(self, e, reads, writes):
        for r in reads:
            self._wait(e, r.w)
        for w in writes:
            self._wait(e, w.w)
            for ev in list(w.r.values()):
                self._wait(e, ev)

    def _mark(self, ev, reads, writes):
        for r in reads:
            old = r.r.get(ev[0].i)
            if old is None or old[1] < ev[1]:
                r.r[ev[0].i] = ev
        for w in writes:
            w.w = ev
            w.r = {}

    def op(self, e, fn, reads=(), writes=(), inc=True):
        self._deps(e, reads, writes)
        inst = fn(self.eng[e])
        self.ninst += 1
        if inc:
            self.ecnt[e] += 1
            inst.then_inc(self.esem[e].h, 1)
            ev = (self.esem[e], self.ecnt[e])
        else:
            ev = (self.esem[e], self.ecnt[e] + 1)
        self._mark(ev, reads, writes)
        return inst

    def dma(self, q, dsem, pairs, reads=(), writes=(), **kw):
        self._deps(q, reads, writes)
        self._wait(q, (dsem, dsem.total))
        for (o, i) in pairs:
            self.eng[q].dma_start(out=o, in_=i, **kw).then_inc(dsem.h, 16)
            dsem.total += 16
            self.ninst += 1
        ev = (dsem, dsem.total)
        self._mark(ev, reads, writes)

    def barrier(self):
        for e in self.eng:
            for o in self.eng:
                if o != e:
                    self._wait(e, (self.esem[o], self.ecnt[o]))
            for d in self.dsems:
                self._wait(e, (d, d.total))


class T:
    _n = [0]

    def __init__(self, P, st, name, shape, dt, psum=False, dma=True):
        T._n[0] += 1
        name = f"t{T._n[0]}_{name}"
        if psum:
            self.t = st.enter_context(P.nc.psum_tensor(name, list(shape), dt))
        else:
            self.t = st.enter_context(P.nc.sbuf_tensor(name, list(shape), dt))
        self.r = Res(name)
        self.d = P.new_dsem("d_" + name) if (dma and not psum) else None
        if self.d is not None and st is not P.stack:
            st.callback(P.release_dsem, self.d)

    def __getitem__(self, k):
        return self.t[k]


class Ring:
    def __init__(self, P, st, name, shape, dt, n, psum=False, dma=True):
        self.tiles = [T(P, st, f"{name}{i}", shape, dt, psum=psum, dma=dma) for i in range(n)]
        self.i = 0

    def next(self):
        t = self.tiles[self.i % len(self.tiles)]
        self.i += 1
        return t


class Cfg:
    def __init__(self, SEQ, PAST, nlayers=4):
        self.SEQ = SEQ
        self.PAST = PAST
        self.T = 16 + SEQ
        self.NT = 1 + SEQ // 128
        self.U = self.NT * 128
        self.NTT = self.NT + 1
        self.UT = self.NTT * 128
        self.NPT = PAST // 128
        self.topk_p = min(256, (self.T - 16) // 4)
        self.topk_s = min(256, (PAST + 64) // 4)
        self.nlayers = nlayers
        self.G = 11


LAYER_KIND = ["A", "B", "C", "A"]
DIN = {"A": 8192, "B": 4608, "C": 6224}
LAM_INIT = [0.8 - 0.6 * float(np.exp(-0.3 * i)) for i in range(4)]


def layer_blocks(kind):
    B = []
    if kind == "A":
        for b in range(4):
            B.append((b * 512, 512, [(0, 512, "rope", dict(hd=128, dup=False, out=None, tdst="qT", tbase=4 * b))]))
        for b in range(4):
            B.append((2048 + b * 512, 512, [(0, 512, "rope", dict(hd=128, dup=False, out=("k", b * 512), tdst="kT", tbase=4 * b))]))
        for b in range(4):
            B.append((4096 + b * 512, 512, [(0, 512, "v", dict(out=("v", b * 512), vcol=b * 512))]))
        for b in range(4):
            B.append((6144 + b * 512, 512, [(0, 512, "z", dict(zcol=b * 512))]))
    elif kind == "B":
        for b in range(4):
            B.append((b * 512, 512, [(0, 512, "rope", dict(hd=64, dup=False, out=None, tdst="qT", tbase=4 * b))]))
        B.append((2048, 512, [(0, 256, "rope", dict(hd=64, dup=True, out=("k", 0), tdst="kT", tbase=0)),
                              (256, 256, "v", dict(out=("v", 0), vcol=0))]))
        for b in range(4):
            B.append((2560 + b * 512, 512, [(0, 512, "z", dict(zcol=b * 512))]))
    else:
        for b in range(4):
            B.append((b * 512, 512, [(0, 512, "rope", dict(hd=128, dup=False, out=None, tdst="qT", tbase=4 * b))]))
        B.append((2048, 512, [(0, 512, "rope", dict(hd=128, dup=False, out=("k", 0), tdst="kT", tbase=0))]))
        B.append((2560, 512, [(0, 512, "v", dict(out=("v", 0), vcol=0))]))
        for b in range(4):
            B.append((3072 + b * 512, 512, [(0, 512, "z", dict(zcol=b * 512))]))
        for b in range(2):
            B.append((5120 + b * 512, 512, [(0, 512, "rope", dict(hd=64, dup=False, out=None, tdst="qiT", tbase=4 * b))]))
        B.append((6144, 80, [(0, 64, "rope", dict(hd=64, dup=True, out=("i", 0), tdst="kiT", tbase=0)),
                             (64, 16, "wi", dict())]))
    return B


def build(cfg):
    nc = bass.Bass("TRN2", target_bir_lowering=False)
    NT, NTT, U, UT, SEQ, PAST, Tn, NPT = cfg.NT, cfg.NTT, cfg.U, cfg.UT, cfg.SEQ, cfg.PAST, cfg.T, cfg.NPT
    D = {}

    def din(name, shape, dt=F32):
        D[name] = nc.dram_tensor(name, list(shape), dt, kind="ExternalInput").ap()

    def dout(name, shape, dt=F32):
        D[name] = nc.dram_tensor(name, list(shape), dt, kind="ExternalOutput").ap()

    def dscr(name, shape, dt):
        D[name] = nc.dram_tensor(name, list(shape), dt, kind="Internal").ap()

    din("xp", [SEQ, 2048]); din("xs", [128, 2048]); din("meta", [16, 2048])
    for l in (0, 3):
        din(f"c{l}k", [2, PAST, 2048]); din(f"c{l}v", [2, PAST, 2048])
    din("c1k", [2, 128, 256]); din("c1v", [2, 128, 256])
    din("c2k", [2, PAST, 512]); din("c2v", [2, PAST, 512]); din("c2i", [2, PAST, 64])
    for l in range(4):
        din(f"normv{l}", [2048]); din(f"win{l}", [2048, DIN[LAYER_KIND[l]]]); din(f"wout{l}", [2048, 2048])
    for l in (0, 3):
        din(f"lam{l}", [128, 4]); din(f"subln{l}", [256])
    din("sinks", [32]); din("fnorm", [2048])
    din("ident", [128, 128], BF16); din("cs128", [UT, 32]); din("cs64", [UT, 16])
    din("dmask", [128, 128], BF16); din("swm", [2, 128, 128], BF16)

    dout("yp", [SEQ, 2048]); dout("ys", [128, 2048])
    for l in (0, 3):
        dout(f"p{l}k", [Tn, 2048]); dout(f"p{l}v", [Tn, 2048]); dout(f"s{l}k", [128, 2048]); dout(f"s{l}v", [128, 2048])
    dout("p1k", [128, 256]); dout("p1v", [128, 256]); dout("s1k", [2, 128, 256]); dout("s1v", [2, 128, 256])
    dout("p2k", [Tn, 512]); dout("p2v", [Tn, 512]); dout("p2i", [Tn, 64])
    dout("s2k", [128, 512]); dout("s2v", [128, 512]); dout("s2i", [128, 64])

    dscr("resid", [UT, 2048], F32)
    dscr("qT", [16, 128, UT], BF16); dscr("kT", [16, 128, UT], BF16)
    dscr("vS", [UT, 2048], BF16); dscr("zS", [UT, 2048], BF16); dscr("ogS", [UT, 2048], BF16)
    dscr("qiT", [8, 128, UT], BF16); dscr("kiT", [1, 128, UT], BF16); dscr("wiS", [UT, 16], F32)
    dscr("mkT", [NT, 128, NT, 128], BF16)
    dscr("mkS", [2, 128, NPT + 1, 64], BF16)
    RD = {k: Res("dram_" + k) for k in D}

    out_written = []

    with ExitStack() as gst:
        P = Prog(nc, gst)
        ident = T(P, gst, "ident", [128, 128], BF16)
        cs128 = T(P, gst, "cs128", [128, NTT, 32], F32)
        cs64 = T(P, gst, "cs64", [128, NTT, 16], F32)
        dmask = T(P, gst, "dmask", [128, 128], BF16)
        swm = T(P, gst, "swm", [128, 2, 128], BF16)
        P.dma("sp", ident.d, [(ident[:], D["ident"])], writes=[ident.r])
        P.dma("sp", cs128.d, [(cs128[:], D["cs128"].rearrange("(t p) c -> p t c", p=128))], writes=[cs128.r])
        P.dma("sp", cs64.d, [(cs64[:], D["cs64"].rearrange("(t p) c -> p t c", p=128))], writes=[cs64.r])
        P.dma("sp", dmask.d, [(dmask[:], D["dmask"])], writes=[dmask.r])
        P.dma("sp", swm.d, [(swm[:], D["swm"].rearrange("j p c -> p j c"))], writes=[swm.r])


        def tile_rows(t):
            return slice(t * 128, (t + 1) * 128)

        def out_rows(t):
            if t == 0:
                return [(slice(112, 128), slice(0, 16))]
            if t < NT:
                return [(slice(0, 128), slice(16 + 128 * (t - 1), 16 + 128 * t))]
            return [(slice(0, 128), slice(0, 128))]

        def inproj(l):
            kind = LAYER_KIND[l]
            blocks = layer_blocks(kind)
            win = D[f"win{l}"]
            with ExitStack() as st:
                G = cfg.G
                xnT = T(P, st, "xnT", [128, 16, G * 128], BF16, dma=False)
                xin = Ring(P, st, "xin", [128, 2048], F32, 2)
                xnb = Ring(P, st, "xnb", [128, 2048], BF16, 2, dma=False)
                junk = T(P, st, "junk", [128, 2048], BF16, dma=False)
                ssr = Ring(P, st, "ss", [128, 1], F32, 2, dma=False)
                wst = Ring(P, st, "wst", [128, 8, 512], F32, 2)
                wbr = Ring(P, st, "wb", [128, 16, 512], BF16, 2, dma=False)
                blkr = Ring(P, st, "blk", [128, 512], F32, 3)
                blkbr = Ring(P, st, "blkb", [128, 512], BF16, 3)
                stg = Ring(P, st, "stg", [128, 4, 128], BF16, 3)
                rtA = T(P, st, "rtA", [128, 64], F32, dma=False)
                rtB = T(P, st, "rtB", [128, 64], F32, dma=False)
                rtC = T(P, st, "rtC", [128, 64], F32, dma=False)
                rtD = T(P, st, "rtD", [128, 64], F32, dma=False)
                tpr = Ring(P, st, "tp", [128, 8, 128], BF16, 2, psum=True)
                mmr = Ring(P, st, "mm", [128, 512], F32, 2, psum=True)
                tqr = Ring(P, st, "tq", [128, 8, 128], BF16, 2, psum=True)
                gbc = T(P, st, "gbc", [128, 2048], F32)
                P.dma("sp", gbc.d, [(gbc[:], D[f"normv{l}"].partition_broadcast(128))], writes=[gbc.r])

                def load_x(t, xt):
                    if l == 0:
                        if t == 0:
                            P.op("pool", lambda e: e.memset(xt[:], 0.0), writes=[xt.r])
                            P.dma("sp", xt.d, [(xt[112:128, :], D["meta"])], writes=[xt.r])
                        elif t < NT:
                            P.dma("sp", xt.d, [(xt[:], D["xp"][(t - 1) * 128:t * 128, :])], writes=[xt.r])
                        else:
                            P.dma("sp", xt.d, [(xt[:], D["xs"])], writes=[xt.r])
                        P.dma("sp", xt.d, [(D["resid"][tile_rows(t), :], xt[:])], reads=[xt.r], writes=[RD["resid"]])
                    else:
                        P.dma("sp", xt.d, [(xt[:], D["resid"][tile_rows(t), :])], reads=[RD["resid"]], writes=[xt.r])

                def seg_rope(t, ps, c0, n, o):
                    hd = o["hd"]; nh = n // hd; half = hd // 8
                    cs = cs128 if hd == 128 else cs64
                    cosb = cs[:, t, 0:half].unsqueeze(1).broadcast_to([128, nh, half])
                    sinb = cs[:, t, half:2 * half].unsqueeze(1).broadcast_to([128, nh, half])
                    blk = blkr.next()
                    ps3 = ps[:, c0:c0 + n].rearrange("p (h d) -> p h d", d=hd)
                    b3 = blk[:, 0:n].rearrange("p (h d) -> p h d", d=hd)
                    A3 = rtA[:, 0:nh * half].rearrange("p (h d) -> p h d", d=half)
                    B3 = rtB[:, 0:nh * half].rearrange("p (h d) -> p h d", d=half)
                    P.op("act", lambda e: e.copy(out=blk[:, 0:n], in_=ps[:, c0:c0 + n]), reads=[ps.r], writes=[blk.r])
                    x1 = b3[:, :, 0:half]; x2 = b3[:, :, half:2 * half]
                    C3 = rtC[:, 0:nh * half].rearrange("p (h d) -> p h d", d=half)
                    D3 = rtD[:, 0:nh * half].rearrange("p (h d) -> p h d", d=half)
                    KR = 7
                    P.op("dve", lambda e: e.tensor_tensor(out=A3, in0=x1, in1=cosb, op=ALU.mult), reads=[blk.r, cs.r], writes=[rtA.r])
                    P.op("dve", lambda e: e.tensor_tensor(out=B3, in0=x2, in1=sinb, op=ALU.mult), reads=[blk.r, cs.r], writes=[rtB.r])
                    P.op("dve", lambda e: e.tensor_tensor(out=C3, in0=x2, in1=cosb, op=ALU.mult), reads=[blk.r, cs.r], writes=[rtC.r])
                    P.op("dve", lambda e: e.tensor_tensor(out=D3, in0=x1, in1=sinb, op=ALU.mult), reads=[blk.r, cs.r], writes=[rtD.r])
                    P.op("dve", lambda e: e.tensor_tensor(out=x1, in0=A3, in1=B3, op=ALU.subtract), reads=[rtA.r, rtB.r], writes=[blk.r])
                    P.op("dve", lambda e: e.tensor_tensor(out=x2, in0=C3, in1=D3, op=ALU.add), reads=[rtC.r, rtD.r], writes=[blk.r])
                    if o["out"] is not None and (KR & 2):
                        which, col = o["out"]
                        store_out(l, which, t, blk, n, col)
                    if not (KR & 4):
                        return
                    bb = blkbr.next()
                    if o["dup"]:
                        bd = bb[:, 0:2 * n].rearrange("p (h two d) -> p h two d", two=2, d=hd)
                        for j in range(2):
                            P.op("pool", lambda e, j=j: e.tensor_copy(out=bd[:, :, j, :], in_=b3), reads=[blk.r], writes=[bb.r])
                        ncol = 2 * n
                    else:
                        P.op("pool", lambda e: e.tensor_copy(out=bb[:, 0:n], in_=blk[:, 0:n]), reads=[blk.r], writes=[bb.r])
                        ncol = n
                    ntr = ncol // 128
                    tq = tqr.next()
                    for i in range(ntr):
                        P.op("pe", lambda e, i=i: e.transpose(out=tq[:, i, :], in_=bb[:, i * 128:(i + 1) * 128], identity=ident[:]),
                             reads=[bb.r, ident.r], writes=[tq.r], inc=(i == ntr - 1))
                    sg = stg.next()
                    P.op("act", lambda e: e.copy(out=sg[:, 0:ntr, :], in_=tq[:, 0:ntr, :]), reads=[tq.r], writes=[sg.r])
                    dst = D[o["tdst"]][o["tbase"]:o["tbase"] + ntr, :, tile_rows(t)].rearrange("i p c -> p i c")
                    P.dma("sp", sg.d, [(dst, sg[:, 0:ntr, :])], reads=[sg.r], writes=[RD[o["tdst"]]])

                def store_out(l, which, t, blk, n, col):
                    if kind == "B":
                        nm = {"k": "1k", "v": "1v"}[which]
                        if t == NT - 1:
                            P.dma("sp", blk.d, [(D["p" + nm][:, :], blk[:, 0:n])], reads=[blk.r], writes=[RD["p" + nm]])
                        elif t == NT:
                            P.dma("sp", blk.d, [(D["s" + nm][b, 64:128, :], blk[b * 64:(b + 1) * 64, 0:n]) for b in range(2)],
                                  reads=[blk.r], writes=[RD["s" + nm]])
                        return
                    nm = f"{l}{which}"
                    dn = ("p" if t < NT else "s") + nm
                    P.dma("sp", blk.d, [(D[dn][rs, col:col + n], blk[ps_, 0:n]) for (ps_, rs) in out_rows(t)],
                          reads=[blk.r], writes=[RD[dn]])

                def seg_v(t, ps, c0, n, o):
                    blk = blkr.next()
                    P.op("act", lambda e: e.copy(out=blk[:, 0:n], in_=ps[:, c0:c0 + n]), reads=[ps.r], writes=[blk.r])
                    which, col = o["out"]
                    store_out(l, which, t, blk, n, col)
                    bb = blkbr.next()
                    P.op("pool", lambda e: e.tensor_copy(out=bb[:, 0:n], in_=blk[:, 0:n]), reads=[blk.r], writes=[bb.r])
                    P.dma("sp", bb.d, [(D["vS"][tile_rows(t), o["vcol"]:o["vcol"] + n], bb[:, 0:n])], reads=[bb.r], writes=[RD["vS"]])

                def seg_z(t, ps, c0, n, o):
                    bb = blkbr.next()
                    P.op("act", lambda e: e.activation(out=bb[:, 0:n], in_=ps[:, c0:c0 + n], func=AF.Silu), reads=[ps.r], writes=[bb.r])
                    P.dma("sp", bb.d, [(D["zS"][tile_rows(t), o["zcol"]:o["zcol"] + n], bb[:, 0:n])], reads=[bb.r], writes=[RD["zS"]])

                def seg_wi(t, ps, c0, n, o):
                    blk = blkr.next()
                    P.op("act", lambda e: e.copy(out=blk[:, 0:n], in_=ps[:, c0:c0 + n]), reads=[ps.r], writes=[blk.r])
                    P.dma("sp", blk.d, [(D["wiS"][tile_rows(t), :], blk[:, 0:n])], reads=[blk.r], writes=[RD["wiS"]])

                SEG = dict(rope=seg_rope, v=seg_v, z=seg_z, wi=seg_wi)

                tiles = list(range(NTT))
                groups = [tiles[i:i + G] for i in range(0, NTT, G)]
                for grp in groups:
                    for s, t in enumerate(grp):
                        xt = xin.next(); ss = ssr.next(); xn = xnb.next()
                        load_x(t, xt)
                        P.op("act", lambda e: e.activation(out=junk[:], in_=xt[:], func=AF.Square, accum_out=ss[:]),
                             reads=[xt.r], writes=[junk.r, ss.r])
                        P.op("dve", lambda e: e.tensor_scalar(out=ss[:], in0=ss[:], scalar1=1.0 / 2048, scalar2=EPS, op0=ALU.mult, op1=ALU.add),
                             reads=[ss.r], writes=[ss.r])
                        P.op("act", lambda e: e.activation(out=ss[:], in_=ss[:], func=AF.Sqrt), reads=[ss.r], writes=[ss.r])
                        P.op("dve", lambda e: e.reciprocal(out=ss[:], in_=ss[:]), reads=[ss.r], writes=[ss.r])
                        P.op("dve", lambda e: e.scalar_tensor_tensor(out=xn[:], in0=xt[:], scalar=ss[:], in1=gbc[:], op0=ALU.mult, op1=ALU.mult),
                             reads=[xt.r, ss.r, gbc.r], writes=[xn.r])
                        for hf in range(2):
                            tp = tpr.next()
                            for i in range(8):
                                kc = hf * 8 + i
                                P.op("pe", lambda e, i=i, kc=kc: e.transpose(out=tp[:, i, :], in_=xn[:, kc * 128:(kc + 1) * 128], identity=ident[:]),
                                     reads=[xn.r, ident.r], writes=[tp.r], inc=(i == 7))
                            P.op("dve", lambda e: e.tensor_copy(out=xnT[:, hf * 8:hf * 8 + 8, s * 128:(s + 1) * 128], in_=tp[:]),
                                 reads=[tp.r], writes=[xnT.r])
                    import os
                    KSUB = int(os.environ.get("KSUB", "99"))
                    if KSUB == 1:
                        continue
                    for (c0, ncb, segs) in blocks:
                        wb = wbr.next()
                        for hf in range(2):
                            ws = wst.next()
                            src = win[hf * 1024:(hf + 1) * 1024, c0:c0 + ncb].rearrange("(k p) n -> p k n", p=128)
                            P.dma("sp", ws.d, [(ws[:, :, 0:ncb], src)], writes=[ws.r])
                            P.op("pool", lambda e: e.tensor_copy(out=wb[:, hf * 8:hf * 8 + 8, 0:ncb], in_=ws[:, :, 0:ncb]),
                                 reads=[ws.r], writes=[wb.r])
                        for s, t in enumerate(grp):
                            ps = mmr.next()
                            for kc in range(16):
                                P.op("pe", lambda e, kc=kc: e.matmul(ps[:, 0:ncb], lhsT=xnT[:, kc, s * 128:(s + 1) * 128], rhs=wb[:, kc, 0:ncb],
                                                                    start=(kc == 0), stop=(kc == 15)),
                                     reads=[xnT.r, wb.r], writes=[ps.r], inc=(kc == 15))
                            for (sc0, sn, sk, so) in segs:
                                if KSUB == 2 or (KSUB == 3 and sk == "rope"):
                                    continue
                                SEG[sk](t, ps, sc0, sn, so)
                P.barrier()
            if kind == "B":
                with ExitStack() as st2:
                    for nm in ("k", "v"):
                        cw = T(P, st2, "cw" + nm, [128, 256], F32)
                        P.dma("sp", cw.d, [(cw[b * 64:(b + 1) * 64, :], D["c1" + nm][b, 64:128, :]) for b in range(2)], writes=[cw.r])
                        P.dma("sp", cw.d, [(D["s1" + nm][b, 0:64, :], cw[b * 64:(b + 1) * 64, :]) for b in range(2)], reads=[cw.r], writes=[RD["s1" + nm]])
                    P.barrier()

        def outproj(l, last):
            with ExitStack() as st:
                wo = T(P, st, "wo", [128, 16, 2048], BF16, dma=False)
                wst = Ring(P, st, "wst", [128, 8, 512], F32, 2)
                ogr = Ring(P, st, "og", [128, 2048], BF16, 2)
                ogT = Ring(P, st, "ogT", [128, 16, 128], BF16, 2, dma=False)
                xin = Ring(P, st, "xin", [128, 2048], F32, 2)
                xo = Ring(P, st, "xo", [128, 2048], F32, 2)
                yo = Ring(P, st, "yo", [128, 2048], F32, 2)
                junk = T(P, st, "junk", [128, 2048], BF16, dma=False)
                ssr = Ring(P, st, "ss", [128, 1], F32, 2, dma=False)
                gfin = T(P, st, "gfin", [128, 2048], F32)
                tpr = Ring(P, st, "tp", [128, 8, 128], BF16, 2, psum=True)
                mmr = Ring(P, st, "mm", [128, 512], F32, 2, psum=True)
                wout = D[f"wout{l}"]
                if last:
                    P.dma("sp", gfin.d, [(gfin[:], D["fnorm"].partition_broadcast(128))], writes=[gfin.r])
                for cb in range(4):
                    for hf in range(2):
                        ws = wst.next()
                        src = wout[hf * 1024:(hf + 1) * 1024, cb * 512:(cb + 1) * 512].rearrange("(k p) n -> p k n", p=128)
                        P.dma("sp", ws.d, [(ws[:], src)], writes=[ws.r])
                        P.op("pool", lambda e: e.tensor_copy(out=wo[:, hf * 8:hf * 8 + 8, cb * 512:(cb + 1) * 512], in_=ws[:]),
                             reads=[ws.r], writes=[wo.r])
                for t in range(NTT):
                    og = ogr.next(); oT = ogT.next(); xt = xin.next(); xn = xo.next()
                    P.dma("sp", og.d, [(og[:], D["ogS"][tile_rows(t), :])], reads=[RD["ogS"]], writes=[og.r])
                    P.dma("sp", xt.d, [(xt[:], D["resid"][tile_rows(t), :])], reads=[RD["resid"]], writes=[xt.r])
                    for hf in range(2):
                        tp = tpr.next()
                        for i in range(8):
                            kc = hf * 8 + i
                            P.op("pe", lambda e, i=i, kc=kc: e.transpose(out=tp[:, i, :], in_=og[:, kc * 128:(kc + 1) * 128], identity=ident[:]),
                                 reads=[og.r, ident.r], writes=[tp.r], inc=(i == 7))
                        P.op("act", lambda e: e.copy(out=oT[:, hf * 8:hf * 8 + 8, :], in_=tp[:]), reads=[tp.r], writes=[oT.r])
                    for cb in range(4):
                        ps = mmr.next()
                        for kc in range(16):
                            P.op("pe", lambda e, kc=kc: e.matmul(ps[:], lhsT=oT[:, kc, :], rhs=wo[:, kc, cb * 512:(cb + 1) * 512],
                                                                start=(kc == 0), stop=(kc == 15)),
                                 reads=[oT.r, wo.r], writes=[ps.r], inc=(kc == 15))
                        P.op("dve", lambda e: e.tensor_tensor(out=xn[:, cb * 512:(cb + 1) * 512], in0=ps[:], in1=xt[:, cb * 512:(cb + 1) * 512], op=ALU.add),
                             reads=[ps.r, xt.r], writes=[xn.r])
                    if t == 0:
                        P.op("pool", lambda e: e.memset(xn[0:112, :], 0.0), writes=[xn.r])
                    if not last:
                        P.dma("sp", xn.d, [(D["resid"][tile_rows(t), :], xn[:])], reads=[xn.r], writes=[RD["resid"]])
                    else:
                        ss = ssr.next(); y = yo.next()
                        P.op("act", lambda e: e.activation(out=junk[:], in_=xn[:], func=AF.Square, accum_out=ss[:]),
                             reads=[xn.r], writes=[junk.r, ss.r])
                        P.op("dve", lambda e: e.tensor_scalar(out=ss[:], in0=ss[:], scalar1=1.0 / 2048, scalar2=EPS, op0=ALU.mult, op1=ALU.add),
                             reads=[ss.r], writes=[ss.r])
                        P.op("act", lambda e: e.activation(out=ss[:], in_=ss[:], func=AF.Sqrt), reads=[ss.r], writes=[ss.r])
                        P.op("dve", lambda e: e.reciprocal(out=ss[:], in_=ss[:]), reads=[ss.r], writes=[ss.r])
                        P.op("dve", lambda e: e.scalar_tensor_tensor(out=y[:], in0=xn[:], scalar=ss[:], in1=gfin[:], op0=ALU.mult, op1=ALU.mult),
                             reads=[xn.r, ss.r, gfin.r], writes=[y.r])
                        if 1 <= t < NT:
                            P.dma("sp", y.d, [(D["yp"][(t - 1) * 128:t * 128, :], y[:])], reads=[y.r], writes=[RD["yp"]])
                        elif t == NT:
                            P.dma("sp", y.d, [(D["ys"][:, :], y[:])], reads=[y.r], writes=[RD["ys"]])
                P.barrier()

        def attn_diff(l):
            scale = 128 ** -0.5
            with ExitStack() as st:
                kts = T(P, st, "kts", [128, 2, U], BF16)
                vaug = T(P, st, "vaug", [128, NT, 257], BF16)
                qtr = Ring(P, st, "qts", [128, 2, 512], BF16, 2)
                ztr = Ring(P, st, "zt", [128, 4, 256], BF16, 2)
                ptr = Ring(P, st, "pt", [128, 512], BF16, 3, dma=False)
                r1 = T(P, st, "r1", [128, 4, 256], F32, dma=False)
                res = T(P, st, "res", [128, 4, 256], F32, dma=False)
                rec = Ring(P, st, "rec", [128, 1], F32, 4, dma=False)
                ssr = Ring(P, st, "ss", [128, 1], F32, 4, dma=False)
                junk = T(P, st, "junk", [128, 256], BF16, dma=False)
                gz = Ring(P, st, "gz", [128, 256], F32, 2, dma=False)
                ogb = Ring(P, st, "ogb", [128, 256], BF16, 3)
                g256 = T(P, st, "g256", [128, 256], F32)
                lamt = T(P, st, "lamt", [128, 4], F32)
                lam2 = T(P, st, "lam2", [128, 2], F32, dma=False)
                nlam = T(P, st, "nlam", [128, 1], F32, dma=False)
                ones = T(P, st, "ones", [128, 128], F32, dma=False)
                cst = Ring(P, st, "cst", [128, NPT, 256], F32, 2)
                cbf = T(P, st, "cbf", [128, NPT, 256], BF16, dma=False)
                ktc = T(P, st, "ktc", [128, 2, PAST + 64], BF16)
                vac = T(P, st, "vac", [128, NPT + 1, 257], BF16)
                qsm = T(P, st, "qsm", [128, 2, 64], BF16)
                zsm = T(P, st, "zsm", [64, 256], BF16)
                str_ = Ring(P, st, "st", [128, 512], F32, 2, psum=True)
                accr = [T(P, st, f"acc{i}", [128, 512], F32, psum=True) for i in range(4)]
                tpk = Ring(P, st, "tpk", [128, 8, 128], BF16, 2, psum=True)

                P.dma("sp", lamt.d, [(lamt[:], D[f"lam{l}"])], writes=[lamt.r])
                P.dma("sp", g256.d, [(g256[:], D[f"subln{l}"].partition_broadcast(128))], writes=[g256.r])
                P.op("dve", lambda e: e.memset(ones[:], 1.0), writes=[ones.r])
                P.op("dve", lambda e: e.tensor_tensor(out=lam2[:], in0=lamt[:].rearrange("p (a b) -> p a b", b=2)[:, :, 0],
                                                      in1=lamt[:].rearrange("p (a b) -> p a b", b=2)[:, :, 1], op=ALU.mult),
                     reads=[lamt.r], writes=[lam2.r])
                acc0 = accr[0]
                P.op("pe", lambda e: e.matmul(acc0[:, 0:2], lhsT=ones[:], rhs=lam2[:], start=True, stop=True), reads=[ones.r, lam2.r], writes=[acc0.r])
                P.op("act", lambda e: e.activation(out=lam2[:], in_=acc0[:, 0:2], func=AF.Exp), reads=[acc0.r], writes=[lam2.r])
                P.op("dve", lambda e: e.tensor_tensor(out=nlam[:], in0=lam2[:, 1:2], in1=lam2[:, 0:1], op=ALU.subtract), reads=[lam2.r], writes=[nlam.r])
                P.op("dve", lambda e: e.tensor_scalar(out=nlam[:], in0=nlam[:], scalar1=-LAM_INIT[l], scalar2=None, op0=ALU.add), reads=[nlam.r], writes=[nlam.r])
                P.op("dve", lambda e: e.tensor_scalar(out=g256[:], in0=g256[:], scalar1=1.0 - LAM_INIT[l], scalar2=None, op0=ALU.mult), reads=[g256.r], writes=[g256.r])
                P.op("dve", lambda e: e.memset(vaug[:, :, 256:257], 1.0), writes=[vaug.r])
                P.op("dve", lambda e: e.memset(vaug[0:112, 0, 256:257], 0.0), writes=[vaug.r])
                P.op("dve", lambda e: e.memset(vac[:, :, 256:257], 1.0), writes=[vac.r])

                def attend(h, qap, nq, ktiles, zap, og_dst_rows):
                    sw = min(nq, 128)
                    nsub = nq // sw
                    for j in range(2):
                        pend = None
                        for ki, kt in enumerate(ktiles):
                            s0 = kt["s0"]; nk = kt["nk"]
                            stp = str_.next()
                            P.op("pe", lambda e: e.matmul(stp[0:nk, s0 * sw:nq], lhsT=kt["k"](j), rhs=qap(j)[:, s0 * sw:nq], start=True, stop=True),
                                 reads=kt["res"] + qap.res, writes=[stp.r])
                            pt = ptr.next()
                            P.op("act", lambda e: e.activation(out=pt[0:nk, s0 * sw:nq], in_=stp[0:nk, s0 * sw:nq], func=AF.Exp, scale=scale),
                                 reads=[stp.r], writes=[pt.r])
                            if kt["diag"]:
                                P.op("pool", lambda e: e.tensor_tensor(out=pt[0:nk, s0 * sw:(s0 + 1) * sw], in0=pt[0:nk, s0 * sw:(s0 + 1) * sw],
                                                                       in1=dmask[0:nk, 0:sw], op=ALU.mult),
                                     reads=[pt.r, dmask.r], writes=[pt.r])
                            def pv(ki=ki, kt=kt, pt=pt, s0=s0, nk=nk):
                                for s in range(s0, nsub):
                                    last = (ki == len(ktiles) - 1) or (ktiles[ki + 1]["s0"] > s)
                                    P.op("pe", lambda e, s=s: e.matmul(accr[s][0:sw, 0:257], lhsT=pt[0:nk, s * sw:(s + 1) * sw], rhs=kt["v"],
                                                                       start=(ki == 0), stop=last),
                                         reads=[pt.r] + kt["res"], writes=[accr[s].r], inc=last)
                            if pend is not None:
                                pend()
                            pend = pv
                        if pend is not None:
                            pend()
                            pend = None
                        for s in range(nsub):
                            rc = rec.next(); a = accr[s]
                            P.op("dve", lambda e: e.tensor_scalar(out=rc[0:sw, :], in0=a[0:sw, 256:257], scalar1=1e-30, scalar2=None, op0=ALU.max),
                                 reads=[a.r], writes=[rc.r])
                            P.op("dve", lambda e: e.reciprocal(out=rc[0:sw, :], in_=rc[0:sw, :]), reads=[rc.r], writes=[rc.r])
                            if j == 0:
                                P.op("dve", lambda e: e.tensor_scalar(out=r1[0:sw, s, :], in0=a[0:sw, 0:256], scalar1=rc[0:sw, :], scalar2=None, op0=ALU.mult),
                                     reads=[a.r, rc.r], writes=[r1.r])
                            else:
                                P.op("dve", lambda e: e.tensor_tensor(out=rc[0:sw, :], in0=rc[0:sw, :], in1=nlam[0:sw, :], op=ALU.mult),
                                     reads=[rc.r, nlam.r], writes=[rc.r])
                                P.op("dve", lambda e: e.scalar_tensor_tensor(out=res[0:sw, s, :], in0=a[0:sw, 0:256], scalar=rc[0:sw, :], in1=r1[0:sw, s, :],
                                                                             op0=ALU.mult, op1=ALU.add),
                                     reads=[a.r, rc.r, r1.r], writes=[res.r])
                    for s in range(nsub):
                        ss = ssr.next(); g = gz.next(); ob = ogb.next()
                        P.op("act", lambda e: e.activation(out=junk[0:sw, :], in_=res[0:sw, s, :], func=AF.Square, accum_out=ss[0:sw, :]),
                             reads=[res.r], writes=[junk.r, ss.r])
                        P.op("dve", lambda e: e.tensor_scalar(out=ss[0:sw, :], in0=ss[0:sw, :], scalar1=1.0 / 256, scalar2=EPS, op0=ALU.mult, op1=ALU.add),
                             reads=[ss.r], writes=[ss.r])
                        P.op("act", lambda e: e.activation(out=ss[0:sw, :], in_=ss[0:sw, :], func=AF.Sqrt), reads=[ss.r], writes=[ss.r])
                        P.op("dve", lambda e: e.reciprocal(out=ss[0:sw, :], in_=ss[0:sw, :]), reads=[ss.r], writes=[ss.r])
                        P.op("pool", lambda e: e.tensor_tensor(out=g[0:sw, :], in0=zap(s), in1=g256[0:sw, :], op=ALU.mult),
                             reads=zap.res + [g256.r], writes=[g.r])
                        P.op("dve", lambda e: e.scalar_tensor_tensor(out=ob[0:sw, :], in0=res[0:sw, s, :], scalar=ss[0:sw, :], in1=g[0:sw, :],
                                                                     op0=ALU.mult, op1=ALU.mult),
                             reads=[res.r, ss.r, g.r], writes=[ob.r])
                        r0 = og_dst_rows(s)
                        P.dma("sp", ob.d, [(D["ogS"][r0:r0 + sw, h * 256:(h + 1) * 256], ob[0:sw, :])], reads=[ob.r], writes=[RD["ogS"]])

                class AP_:
                    def __init__(self, fn, res):
                        self.fn = fn; self.res = res

                    def __call__(self, *a):
                        return self.fn(*a)

                nqb = (NT + 3) // 4
                for h in range(8):
                    P.dma("sp", kts.d, [(kts[:, j, :], D["kT"][2 * h + j, :, 0:U]) for j in range(2)], reads=[RD["kT"]], writes=[kts.r])
                    P.dma("sp", vaug.d, [(vaug[:, :, 0:256], D["vS"][0:U, h * 256:(h + 1) * 256].rearrange("(t p) e -> p t e", p=128))],
                          reads=[RD["vS"]], writes=[vaug.r])
                    for qb in range(nqb):
                        t0 = qb * 4; nt = min(4, NT - t0); nq = nt * 128
                        qt = qtr.next(); zt = ztr.next()
                        P.dma("sp", qt.d, [(qt[:, j, 0:nq], D["qT"][2 * h + j, :, t0 * 128:t0 * 128 + nq]) for j in range(2)],
                              reads=[RD["qT"]], writes=[qt.r])
                        P.dma("sp", zt.d, [(zt[:, 0:nt, :], D["zS"][t0 * 128:t0 * 128 + nq, h * 256:(h + 1) * 256].rearrange("(t p) e -> p t e", p=128))],
                              reads=[RD["zS"]], writes=[zt.r])
                        ktl = []
                        for kt in range(t0 + nt):
                            ktl.append(dict(k=(lambda j, kt=kt: kts[:, j, kt * 128:(kt + 1) * 128]), v=vaug[:, kt, :], nk=128,
                                            s0=max(kt - t0, 0), diag=(kt >= t0), res=[kts.r, vaug.r]))
                        attend(h, AP_(lambda j, qt=qt: qt[:, j, :], [qt.r]), nq, ktl,
                               AP_(lambda s, zt=zt: zt[:, s, :], [zt.r]), lambda s, t0=t0: (t0 + s) * 128)
                    for b in range(2):
                        for which, dst in (("k", None), ("v", vac)):
                            cs_ = cst.next()
                            src = D[f"c{l}{which}"][b, :, h * 256:(h + 1) * 256].rearrange("(t p) e -> p t e", p=128)
                            P.dma("sp", cs_.d, [(cs_[:], src)], writes=[cs_.r])
                            if which == "k":
                                P.op("pool", lambda e: e.tensor_copy(out=cbf[:], in_=cs_[:]), reads=[cs_.r], writes=[cbf.r])
                                for j in range(2):
                                    for t8 in range(0, NPT, 8):
                                        n8 = min(8, NPT - t8)
                                        tp = tpk.next()
                                        for i in range(n8):
                                            P.op("pe", lambda e, i=i: e.transpose(out=tp[:, i, :], in_=cbf[:, t8 + i, j * 128:(j + 1) * 128], identity=ident[:]),
                                                 reads=[cbf.r, ident.r], writes=[tp.r], inc=(i == n8 - 1))
                                        P.op("act", lambda e: e.copy(out=ktc[:, j, t8 * 128:(t8 + n8) * 128], in_=tp[:, 0:n8, :]),
                                             reads=[tp.r], writes=[ktc.r])
                            else:
                                P.op("pool", lambda e: e.tensor_copy(out=vac[:, 0:NPT, 0:256], in_=cs_[:]), reads=[cs_.r], writes=[vac.r])
                        sr = NT * 128 + b * 64
                        P.dma("sp", ktc.d, [(ktc[:, j, PAST:PAST + 64], D["kT"][2 * h + j, :, sr:sr + 64]) for j in range(2)],
                              reads=[RD["kT"]], writes=[ktc.r])
                        P.dma("sp", vac.d, [(vac[0:64, NPT, 0:256], D["vS"][sr:sr + 64, h * 256:(h + 1) * 256])], reads=[RD["vS"]], writes=[vac.r])
                        P.dma("sp", qsm.d, [(qsm[:, j, :], D["qT"][2 * h + j, :, sr:sr + 64]) for j in range(2)], reads=[RD["qT"]], writes=[qsm.r])
                        P.dma("sp", zsm.d, [(zsm[:], D["zS"][sr:sr + 64, h * 256:(h + 1) * 256])], reads=[RD["zS"]], writes=[zsm.r])
                        ktl = []
                        for kt in range(NPT + 1):
                            nk = 128 if kt < NPT else 64
                            ktl.append(dict(k=(lambda j, kt=kt, nk=nk: ktc[:, j, kt * 128:kt * 128 + nk]), v=vac[0:nk, kt, :], nk=nk,
                                            s0=0, diag=False, res=[ktc.r, vac.r]))
                        attend(h, AP_(lambda j: qsm[:, j, :], [qsm.r]), 64, ktl, AP_(lambda s: zsm[:, :], [zsm.r]), lambda s, sr=sr: sr)
                P.barrier()

        class AP_:
            def __init__(self, fn, res):
                self.fn = fn; self.res = res

            def __call__(self, *a):
                return self.fn(*a)

        def attn_swa(l):
            scale = 64 ** -0.5
            with ExitStack() as st:
                kT2 = T(P, st, "kT2", [128, 4, U], BF16)
                vaug = T(P, st, "vaug", [128, NT, 4, 65], BF16)
                qtr = Ring(P, st, "qt", [128, 16, 128], BF16, 2)
                ztr = Ring(P, st, "zt", [128, 2048], BF16, 2)
                ogr = Ring(P, st, "ogt", [128, 2048], BF16, 2)
                ptr = Ring(P, st, "pt", [128, 4, 128], BF16, 4, dma=False)
                oacc = Ring(P, st, "oacc", [128, 4, 65], F32, 2, dma=False)
                den = Ring(P, st, "den", [128, 4], F32, 2, dma=False)
                otmp = Ring(P, st, "otmp", [128, 4, 64], F32, 2, dma=False)
                esink = T(P, st, "esink", [128, 32], F32)
                cst = T(P, st, "cst", [128, 256], F32)
                cbd = T(P, st, "cbd", [128, 512], BF16, dma=False)
                kTs = T(P, st, "kTs", [128, 4, 192], BF16)
                vas = T(P, st, "vas", [128, 2, 4, 65], BF16)
                vst = T(P, st, "vst", [128, 4, 64], F32)
                vnb = T(P, st, "vnb", [64, 4, 64], BF16)
                str_ = Ring(P, st, "st", [128, 4, 128], F32, 4, psum=True)
                accr = Ring(P, st, "acc", [128, 512], F32, 2, psum=True)
                tpk = Ring(P, st, "tpk", [128, 8, 128], BF16, 1, psum=True)

                P.dma("sp", esink.d, [(esink[:], D["sinks"].partition_broadcast(128))], writes=[esink.r])
                P.op("act", lambda e: e.activation(out=esink[:], in_=esink[:], func=AF.Exp), reads=[esink.r], writes=[esink.r])
                P.dma("sp", kT2.d, [(kT2[:], D["kT"][0:4, :, 0:U].rearrange("i p c -> p i c"))], reads=[RD["kT"]], writes=[kT2.r])
                P.op("dve", lambda e: e.memset(vaug[:].rearrange("p t g c -> p (t g) c")[:, :, 64:65], 1.0), writes=[vaug.r])
                P.op("dve", lambda e: e.memset(vaug[0:112, 0, :, 64:65], 0.0), writes=[vaug.r])
                P.dma("sp", vaug.d, [(vaug[:, t, :, 0:64], D["vS"][t * 128:(t + 1) * 128, 0:256].rearrange("p (g d) -> p g d", d=64)) for t in range(NT)],
                      reads=[RD["vS"]], writes=[vaug.r])

                def v3(tl, nq):
                    flat = tl[:].rearrange("p i c -> p (i c)")[:, 0:4 * nq]
                    return flat, flat.rearrange("p (i c) -> p i c", c=nq)

                def unit_tile(qt, nq, ktiles, zt, ogt, qcompact=False):
                    q3 = qt[:].rearrange("p i c -> p (i c)")[:, 0:16 * nq].rearrange("p (i c) -> p i c", c=nq) if qcompact else qt[:]
                    for g in range(4):
                        for hf in range(2):
                            h0 = 8 * g + 4 * hf
                            pts = []
                            for (kfn, vfn, nk, mk, kres) in ktiles:
                                stAB = [str_.next(), str_.next()]
                                sAB = [v3(x_, nq) for x_ in stAB]
                                for i in range(4):
                                    h = h0 + i; pr = h // 2; half = h % 2
                                    P.op("pe", lambda e, i=i, pr=pr, half=half: e.matmul(sAB[half][1][0:nk, i // 2, :], lhsT=kfn(g, half),
                                                                                         rhs=q3[64 * half:64 * half + 64, pr, :], start=True, stop=True),
                                         reads=kres + [qt.r], writes=[stAB[half].r], inc=(i >= 2))
                                pt = ptr.next()
                                pf, p3 = v3(pt, nq)
                                for half in range(2):
                                    P.op("act", lambda e, half=half: e.activation(out=pf[0:nk, half * 2 * nq:(half + 1) * 2 * nq], in_=sAB[half][0][0:nk, 0:2 * nq], func=AF.Exp, scale=scale),
                                         reads=[stAB[half].r], writes=[pt.r])
                                if mk is not None:
                                    P.op("dve", lambda e: e.tensor_tensor(out=p3[0:nk], in0=p3[0:nk],
                                                                          in1=mk.unsqueeze(1).broadcast_to([nk, 4, nq]), op=ALU.mult),
                                         reads=[pt.r, swm.r], writes=[pt.r])
                                pts.append(p3)
                                pts[-1] = (p3, pt)
                            acc = accr.next()
                            a3 = acc[:, 0:260].rearrange("p (i c) -> p i c", c=65)
                            for i in range(4):
                                for ki, (kfn, vfn, nk, mk, kres) in enumerate(ktiles):
                                    last = ki == len(ktiles) - 1
                                    P.op("pe", lambda e, i=i, ki=ki: e.matmul(a3[0:nq, i, :], lhsT=pts[ki][0][0:nk, (i % 2) * 2 + i // 2, :], rhs=vfn(g),
                                                                              start=(ki == 0), stop=last),
                                         reads=[pts[ki][1].r] + kres, writes=[acc.r], inc=(last and i == 3))
                            oa = oacc.next(); dn = den.next()
                            P.op("act", lambda e: e.copy(out=oa[0:nq].rearrange("p i c -> p (i c)"), in_=acc[0:nq, 0:260]), reads=[acc.r], writes=[oa.r])
                            P.op("dve", lambda e: e.tensor_tensor(out=dn[0:nq, :], in0=oa[0:nq, :, 64], in1=esink[0:nq, h0:h0 + 4], op=ALU.add),
                                 reads=[oa.r, esink.r], writes=[dn.r])
                            P.op("dve", lambda e: e.reciprocal(out=dn[0:nq, :], in_=dn[0:nq, :]), reads=[dn.r], writes=[dn.r])
                            for i in range(4):
                                h = h0 + i
                                P.op("dve", lambda e, i=i, h=h: e.scalar_tensor_tensor(out=ogt[0:nq, h * 64:(h + 1) * 64], in0=oa[0:nq, i, 0:64], scalar=dn[0:nq, i:i + 1],
                                                                                       in1=zt[0:nq, h * 64:(h + 1) * 64], op0=ALU.mult, op1=ALU.mult),
                                     reads=[oa.r, dn.r, zt.r], writes=[ogt.r])

                for m in range(NT):
                    qt = qtr.next(); zt = ztr.next(); ogt = ogr.next()
                    P.dma("sp", qt.d, [(qt[:], D["qT"][0:16, :, tile_rows(m)].rearrange("i p c -> p i c"))], reads=[RD["qT"]], writes=[qt.r])
                    P.dma("sp", zt.d, [(zt[:], D["zS"][tile_rows(m), :])], reads=[RD["zS"]], writes=[zt.r])
                    ktl = []
                    for j, kt in enumerate((m - 1, m)):
                        if kt < 0:
                            continue
                        ktl.append(((lambda g, half, kt=kt: kT2[64 * half:64 * half + 64, g, kt * 128:(kt + 1) * 128]),
                                    (lambda g, kt=kt: vaug[:, kt, g, :]), 128, swm[:, j, :], [kT2.r, vaug.r]))
                    unit_tile(qt, 128, ktl, zt, ogt)
                    P.dma("sp", ogt.d, [(D["ogS"][tile_rows(m), :], ogt[:])], reads=[ogt.r], writes=[RD["ogS"]])
                P.op("dve", lambda e: e.memset(vas[:].rearrange("p t g c -> p (t g) c")[:, :, 64:65], 1.0), writes=[vas.r])
                for b in range(2):
                    sr = NT * 128 + b * 64
                    P.dma("sp", cst.d, [(cst[:], D["c1k"][b])], writes=[cst.r])
                    cd = cbd[:].rearrange("p (g two d) -> p g two d", two=2, d=64)
                    for j in range(2):
                        P.op("pool", lambda e, j=j: e.tensor_copy(out=cd[:, :, j, :], in_=cst[:].rearrange("p (g d) -> p g d", d=64)), reads=[cst.r], writes=[cbd.r])
                    tp = tpk.next()
                    for g in range(4):
                        P.op("pe", lambda e, g=g: e.transpose(out=tp[:, g, :], in_=cbd[:, g * 128:(g + 1) * 128], identity=ident[:]),
                             reads=[cbd.r, ident.r], writes=[tp.r], inc=(g == 3))
                    P.op("act", lambda e: e.copy(out=kTs[:, :, 0:128], in_=tp[:, 0:4, :]), reads=[tp.r], writes=[kTs.r])
                    P.dma("sp", kTs.d, [(kTs[:, :, 128:192], D["kT"][0:4, :, sr:sr + 64].rearrange("i p c -> p i c"))], reads=[RD["kT"]], writes=[kTs.r])
                    P.dma("sp", vst.d, [(vst[:], D["c1v"][b].rearrange("p (g d) -> p g d", d=64))], writes=[vst.r])
                    P.op("pool", lambda e: e.tensor_copy(out=vas[:, 0, :, 0:64], in_=vst[:]), reads=[vst.r], writes=[vas.r])
                    P.dma("sp", vnb.d, [(vnb[:], D["vS"][sr:sr + 64, 0:256].rearrange("p (g d) -> p g d", d=64))], reads=[RD["vS"]], writes=[vnb.r])
                    P.op("pool", lambda e: e.tensor_copy(out=vas[0:64, 1, :, 0:64], in_=vnb[:]), reads=[vnb.r], writes=[vas.r])
                    qt = qtr.next(); zt = ztr.next(); ogt = ogr.next()
                    P.dma("sp", qt.d, [(qt[:].rearrange("p i c -> p (i c)")[:, 0:1024].rearrange("p (i c) -> p i c", c=64), D["qT"][0:16, :, sr:sr + 64].rearrange("i p c -> p i c"))], reads=[RD["qT"]], writes=[qt.r])
                    P.dma("sp", zt.d, [(zt[0:64, :], D["zS"][sr:sr + 64, :])], reads=[RD["zS"]], writes=[zt.r])
                    ktl = [((lambda g, half: kTs[64 * half:64 * half + 64, g, 0:128]), (lambda g: vas[:, 0, g, :]), 128, None, [kTs.r, vas.r]),
                           ((lambda g, half: kTs[64 * half:64 * half + 64, g, 128:192]), (lambda g: vas[0:64, 1, g, :]), 64, None, [kTs.r, vas.r])]
                    unit_tile(qt, 64, ktl, zt, ogt, qcompact=True)
                    P.dma("sp", ogt.d, [(D["ogS"][sr:sr + 64, :], ogt[0:64, :])], reads=[ogt.r], writes=[RD["ogS"]])
                P.barrier()

        def attn_dsa(l):
            scale = 128 ** -0.5
            NIT = 14
            with ExitStack() as st:
                WMAX = max(U, PAST + 64)
                kiT2 = T(P, st, "kiT2", [128, WMAX], BF16)
                sc = T(P, st, "sc", [128, WMAX], F32, dma=False)
                junkb = T(P, st, "junkb", [128, WMAX], BF16, dma=False)
                mq = T(P, st, "mq", [128, WMAX], BF16, dma=False)
                qir = Ring(P, st, "qi", [128, 8, 128], BF16, 2)
                wir = Ring(P, st, "wi", [128, 16], F32, 2)
                Dg = T(P, st, "Dg", [128, 16, 128], BF16, dma=False)
                rhr = Ring(P, st, "rh", [128, 512], BF16, 4, dma=False)
                identf = T(P, st, "identf", [128, 128], F32, dma=False)
                sm = {k: T(P, st, "b_" + k, [128, 1], F32, dma=False) for k in ("lo", "hi", "mid", "cnt", "pred", "d1", "d2")}
                mstg = Ring(P, st, "mstg", [128, 8, 128], BF16, 2)
                cis = T(P, st, "cis", [128, NPT, 64], F32)
                cib = T(P, st, "cib", [128, NPT, 128], BF16, dma=False)
                yr = Ring(P, st, "y", [128, 512], F32, 4, psum=True)
                scp = Ring(P, st, "scp", [128, 512], F32, 2, psum=True)
                tpm = Ring(P, st, "tpm", [128, 8, 128], BF16, 2, psum=True)
                P.op("dve", lambda e: e.tensor_copy(out=identf[:], in_=ident[:]), reads=[ident.r], writes=[identf.r])

                def index_tile(qi, wi, nq, nkeys, topk, pad_cols, diag, mask_dst):
                    for h in range(16):
                        P.op("dve", lambda e, h=h: e.tensor_scalar(out=Dg[0:nq, h, 0:nq], in0=identf[0:nq, 0:nq], scalar1=wi[0:nq, h:h + 1], scalar2=None, op0=ALU.mult),
                             reads=[identf.r, wi.r], writes=[Dg.r])
                    nkb = (nkeys + 511) // 512
                    for kb in range(nkb):
                        k0 = kb * 512; nk = min(512, nkeys - k0)
                        sp_ = scp.next()
                        rhs_ = []

                        def dg(h):
                            P.op("pe", lambda e: e.matmul(sp_[0:nq, 0:nk], lhsT=Dg[0:nq, h, 0:nq], rhs=rhs_[h][0:nq, 0:nk], start=(h == 0), stop=(h == 15)),
                                 reads=[Dg.r, rhs_[h].r], writes=[sp_.r], inc=(h == 15))
                        for h in range(16):
                            pr = h // 2; half = h % 2
                            y = yr.next()
                            P.op("pe", lambda e: e.matmul(y[0:nq, 0:nk], lhsT=qi[64 * half:64 * half + 64, pr, 0:nq], rhs=kiT2[64 * half:64 * half + 64, k0:k0 + nk],
                                                          start=True, stop=True), reads=[qi.r, kiT2.r], writes=[y.r])
                            rh = rhr.next()
                            if h % 2 == 0:
                                P.op("act", lambda e: e.activation(out=rh[0:nq, 0:nk], in_=y[0:nq, 0:nk], func=AF.Relu), reads=[y.r], writes=[rh.r])
                            else:
                                P.op("dve", lambda e: e.tensor_scalar(out=rh[0:nq, 0:nk], in0=y[0:nq, 0:nk], scalar1=0.0, scalar2=None, op0=ALU.max), reads=[y.r], writes=[rh.r])
                            rhs_.append(rh)
                            if h >= 2:
                                dg(h - 2)
                        dg(14); dg(15)
                        P.op("act", lambda e: e.copy(out=sc[0:nq, k0:k0 + nk], in_=sp_[0:nq, 0:nk]), reads=[sp_.r], writes=[sc.r])
                    lo, hi, mid, cnt, pred, d1, d2 = (sm[k] for k in ("lo", "hi", "mid", "cnt", "pred", "d1", "d2"))
                    P.op("dve", lambda e: e.tensor_reduce(out=hi[0:nq, :], in_=sc[0:nq, 0:nkeys], axis=mybir.AxisListType.X, op=ALU.max), reads=[sc.r], writes=[hi.r])
                    P.op("dve", lambda e: e.tensor_reduce(out=lo[0:nq, :], in_=sc[0:nq, 0:nkeys], axis=mybir.AxisListType.X, op=ALU.min), reads=[sc.r], writes=[lo.r])
                    P.op("dve", lambda e: e.tensor_scalar(out=lo[0:nq, :], in0=lo[0:nq, :], scalar1=-1.0, scalar2=None, op0=ALU.add), reads=[lo.r], writes=[lo.r])
                    if pad_cols:
                        P.op("dve", lambda e: e.memset(sc[0:nq, 0:pad_cols], NEGBIG), writes=[sc.r])
                    if diag:
                        P.op("dve", lambda e: e.memset(sc[0:64, nkeys - 64:nkeys], NEGBIG), writes=[sc.r])
                    for it in range(NIT):
                        P.op("dve", lambda e: e.tensor_tensor(out=mid[0:nq, :], in0=lo[0:nq, :], in1=hi[0:nq, :], op=ALU.add), reads=[lo.r, hi.r], writes=[mid.r])
                        P.op("dve", lambda e: e.tensor_scalar(out=mid[0:nq, :], in0=mid[0:nq, :], scalar1=0.5, scalar2=None, op0=ALU.mult), reads=[mid.r], writes=[mid.r])
                        P.op("dve", lambda e: e.tensor_scalar(out=junkb[0:nq, 0:nkeys], in0=sc[0:nq, 0:nkeys], scalar1=mid[0:nq, :], scalar2=None,
                                                              op0=ALU.is_ge, op1=ALU.add, accum_out=cnt[0:nq, :]),
                             reads=[sc.r, mid.r], writes=[junkb.r, cnt.r])
                        P.op("dve", lambda e: e.tensor_scalar(out=pred[0:nq, :], in0=cnt[0:nq, :], scalar1=float(topk) - 0.5, scalar2=None, op0=ALU.is_ge), reads=[cnt.r], writes=[pred.r])
                        P.op("dve", lambda e: e.tensor_tensor(out=d1[0:nq, :], in0=mid[0:nq, :], in1=lo[0:nq, :], op=ALU.subtract), reads=[mid.r, lo.r], writes=[d1.r])
                        P.op("dve", lambda e: e.tensor_tensor(out=d2[0:nq, :], in0=hi[0:nq, :], in1=mid[0:nq, :], op=ALU.subtract), reads=[mid.r, hi.r], writes=[d2.r])
                        P.op("dve", lambda e: e.scalar_tensor_tensor(out=lo[0:nq, :], in0=d1[0:nq, :], scalar=pred[0:nq, :], in1=lo[0:nq, :], op0=ALU.mult, op1=ALU.add),
                             reads=[d1.r, pred.r, lo.r], writes=[lo.r])
                        P.op("dve", lambda e: e.scalar_tensor_tensor(out=hi[0:nq, :], in0=d2[0:nq, :], scalar=pred[0:nq, :], in1=mid[0:nq, :], op0=ALU.mult, op1=ALU.add),
                             reads=[d2.r, pred.r, mid.r], writes=[hi.r])
                    P.op("dve", lambda e: e.tensor_scalar(out=mq[0:nq, 0:nkeys], in0=sc[0:nq, 0:nkeys], scalar1=lo[0:nq, :], scalar2=None, op0=ALU.is_ge),
                         reads=[sc.r, lo.r], writes=[mq.r])
                    nkt = (nkeys + 127) // 128
                    for t8 in range(0, nkt, 8):
                        n8 = min(8, nkt - t8)
                        tp = tpm.next()
                        for i in range(n8):
                            kk = min(128, nkeys - (t8 + i) * 128)
                            P.op("pe", lambda e, i=i, kk=kk: e.transpose(out=tp[0:kk, i, 0:nq], in_=mq[0:nq, (t8 + i) * 128:(t8 + i) * 128 + kk], identity=ident[0:nq, 0:nq]),
                                 reads=[mq.r, ident.r], writes=[tp.r], inc=(i == n8 - 1))
                        sg = mstg.next()
                        P.op("act", lambda e: e.copy(out=sg[:, 0:n8, 0:nq], in_=tp[:, 0:n8, 0:nq]), reads=[tp.r], writes=[sg.r])
                        mask_dst(t8, n8, sg)

                P.dma("sp", kiT2.d, [(kiT2[:, 0:U], D["kiT"][0, :, 0:U])], reads=[RD["kiT"]], writes=[kiT2.r])
                for m in range(NT):
                    qi = qir.next(); wi = wir.next()
                    P.dma("sp", qi.d, [(qi[:], D["qiT"][0:8, :, tile_rows(m)].rearrange("i p c -> p i c"))], reads=[RD["qiT"]], writes=[qi.r])
                    P.dma("sp", wi.d, [(wi[:], D["wiS"][tile_rows(m), :])], reads=[RD["wiS"]], writes=[wi.r])

                    def mdst(t8, n8, sg, m=m):
                        P.dma("sp", sg.d, [(D["mkT"][m, :, t8:t8 + n8, :], sg[:, 0:n8, :])], reads=[sg.r], writes=[RD["mkT"]])
                    index_tile(qi, wi, 128, (m + 1) * 128, cfg.topk_p, 112, True, mdst)
                for b in range(2):
                    sr = NT * 128 + b * 64
                    P.dma("sp", cis.d, [(cis[:], D["c2i"][b].rearrange("(t p) d -> p t d", p=128))], writes=[cis.r])
                    for j in range(2):
                        P.op("pool", lambda e, j=j: e.tensor_copy(out=cib[:, :, j * 64:(j + 1) * 64], in_=cis[:]), reads=[cis.r], writes=[cib.r])
                    for t8 in range(0, NPT, 8):
                        n8 = min(8, NPT - t8)
                        tp = tpm.next()
                        for i in range(n8):
                            P.op("pe", lambda e, i=i: e.transpose(out=tp[:, i, :], in_=cib[:, t8 + i, :], identity=ident[:]), reads=[cib.r, ident.r], writes=[tp.r], inc=(i == n8 - 1))
                        P.op("act", lambda e: e.copy(out=kiT2[:, t8 * 128:(t8 + n8) * 128], in_=tp[:, 0:n8, :]), reads=[tp.r], writes=[kiT2.r])
                    P.dma("sp", kiT2.d, [(kiT2[:, PAST:PAST + 64], D["kiT"][0, :, sr:sr + 64])], reads=[RD["kiT"]], writes=[kiT2.r])
                    qi = qir.next(); wi = wir.next()
                    P.dma("sp", qi.d, [(qi[:, :, 0:64], D["qiT"][0:8, :, sr:sr + 64].rearrange("i p c -> p i c"))], reads=[RD["qiT"]], writes=[qi.r])
                    P.dma("sp", wi.d, [(wi[0:64, :], D["wiS"][sr:sr + 64, :])], reads=[RD["wiS"]], writes=[wi.r])

                    def mdst_s(t8, n8, sg, b=b):
                        P.dma("sp", sg.d, [(D["mkS"][b, :, t8:t8 + n8, :], sg[:, 0:n8, 0:64])], reads=[sg.r], writes=[RD["mkS"]])
                    index_tile(qi, wi, 64, PAST + 64, cfg.topk_s, 0, False, mdst_s)
                P.barrier()
            with ExitStack() as st:
                WMAX = max(U, PAST + 64)
                NKT = max(NT, NPT + 1)
                kTg = T(P, st, "kTg", [128, WMAX], BF16)
                vag = T(P, st, "vag", [128, NKT, 129], BF16)
                qtr = Ring(P, st, "qt", [128, 4, 128], BF16, 2)
                mkr = Ring(P, st, "mk", [128, NKT, 128], BF16, 2)
                ztr = Ring(P, st, "zt", [128, 512], BF16, 2)
                ogr = Ring(P, st, "ogt", [128, 512], BF16, 2)
                ptr = Ring(P, st, "pt", [128, 4, 128], BF16, 3, dma=False)
                rec = Ring(P, st, "rec", [128, 1], F32, 4, dma=False)
                cst = T(P, st, "cst", [128, NPT, 128], F32)
                cbf = T(P, st, "cbf", [128, NPT, 128], BF16, dma=False)
                str_ = Ring(P, st, "st", [128, 4, 128], F32, 2, psum=True)
                accr = [T(P, st, f"acc{i}", [128, 512], F32, psum=True) for i in range(4)]
                tpk = Ring(P, st, "tpk", [128, 8, 128], BF16, 2, psum=True)

                def v3(tl, nq):
                    flat = tl[:].rearrange("p i c -> p (i c)")[:, 0:4 * nq]
                    return flat, flat.rearrange("p (i c) -> p i c", c=nq)

                def att_tile(g, qt, nq, nkt, nk_last, mk, zt, ogt):
                    qf, q3 = v3(qt, nq)
                    pend = None
                    for kt in range(nkt):
                        nk = 128 if kt < nkt - 1 else nk_last
                        stp = str_.next(); pt = ptr.next()
                        sf, s3 = v3(stp, nq); pf, p3 = v3(pt, nq)
                        P.op("pe", lambda e: e.matmul(sf[0:nk, :], lhsT=kTg[:, kt * 128:kt * 128 + nk], rhs=qf, start=True, stop=True),
                             reads=[kTg.r, qt.r], writes=[stp.r])
                        P.op("act", lambda e: e.activation(out=pf[0:nk, :], in_=sf[0:nk, :], func=AF.Exp, scale=scale), reads=[stp.r], writes=[pt.r])
                        P.op("dve", lambda e: e.tensor_tensor(out=p3[0:nk], in0=p3[0:nk],
                                                              in1=mk[0:nk, kt, 0:nq].unsqueeze(1).broadcast_to([nk, 4, nq]), op=ALU.mult),
                             reads=[pt.r, mk.r], writes=[pt.r])
                        def pv(kt=kt, nk=nk, p3=p3, pt=pt):
                            for i in range(4):
                                P.op("pe", lambda e, i=i: e.matmul(accr[i][0:nq, 0:129], lhsT=p3[0:nk, i, :], rhs=vag[0:nk, kt, :], start=(kt == 0), stop=(kt == nkt - 1)),
                                     reads=[pt.r, vag.r], writes=[accr[i].r], inc=(kt == nkt - 1))
                        if pend is not None:
                            pend()
                        pend = pv
                    pend()
                    for i in range(4):
                        rc = rec.next(); a = accr[i]
                        P.op("dve", lambda e: e.tensor_scalar(out=rc[0:nq, :], in0=a[0:nq, 128:129], scalar1=1e-30, scalar2=None, op0=ALU.max), reads=[a.r], writes=[rc.r])
                        P.op("dve", lambda e: e.reciprocal(out=rc[0:nq, :], in_=rc[0:nq, :]), reads=[rc.r], writes=[rc.r])
                        P.op("dve", lambda e: e.scalar_tensor_tensor(out=ogt[0:nq, i * 128:(i + 1) * 128], in0=a[0:nq, 0:128], scalar=rc[0:nq, :],
                                                                     in1=zt[0:nq, i * 128:(i + 1) * 128], op0=ALU.mult, op1=ALU.mult),
                             reads=[a.r, rc.r, zt.r], writes=[ogt.r])

                P.op("dve", lambda e: e.memset(vag[:, :, 128:129], 1.0), writes=[vag.r])
                for g in range(4):
                    P.dma("sp", kTg.d, [(kTg[:, 0:U], D["kT"][g, :, 0:U])], reads=[RD["kT"]], writes=[kTg.r])
                    P.dma("sp", vag.d, [(vag[:, 0:NT, 0:128], D["vS"][0:U, g * 128:(g + 1) * 128].rearrange("(t p) e -> p t e", p=128))], reads=[RD["vS"]], writes=[vag.r])
                    P.op("dve", lambda e: e.memset(vag[:, :, 128:129], 1.0), writes=[vag.r])
                    P.op("dve", lambda e: e.memset(vag[0:112, 0, 128:129], 0.0), writes=[vag.r])
                    for m in range(NT):
                        qt = qtr.next(); mk = mkr.next(); zt = ztr.next(); ogt = ogr.next()
                        P.dma("sp", qt.d, [(qt[:], D["qT"][4 * g:4 * g + 4, :, tile_rows(m)].rearrange("i p c -> p i c"))], reads=[RD["qT"]], writes=[qt.r])
                        P.dma("sp", mk.d, [(mk[:, 0:m + 1, :], D["mkT"][m, :, 0:m + 1, :])], reads=[RD["mkT"]], writes=[mk.r])
                        P.dma("sp", zt.d, [(zt[:], D["zS"][tile_rows(m), g * 512:(g + 1) * 512])], reads=[RD["zS"]], writes=[zt.r])
                        att_tile(g, qt, 128, m + 1, 128, mk, zt, ogt)
                        P.dma("sp", ogt.d, [(D["ogS"][tile_rows(m), g * 512:(g + 1) * 512], ogt[:])], reads=[ogt.r], writes=[RD["ogS"]])
                    for b in range(2):
                        sr = NT * 128 + b * 64
                        P.dma("sp", cst.d, [(cst[:], D["c2k"][b, :, g * 128:(g + 1) * 128].rearrange("(t p) d -> p t d", p=128))], writes=[cst.r])
                        P.op("pool", lambda e: e.tensor_copy(out=cbf[:], in_=cst[:]), reads=[cst.r], writes=[cbf.r])
                        for t8 in range(0, NPT, 8):
                            n8 = min(8, NPT - t8)
                            tp = tpk.next()
                            for i in range(n8):
                                P.op("pe", lambda e, i=i: e.transpose(out=tp[:, i, :], in_=cbf[:, t8 + i, :], identity=ident[:]), reads=[cbf.r, ident.r], writes=[tp.r], inc=(i == n8 - 1))
                            P.op("act", lambda e: e.copy(out=kTg[:, t8 * 128:(t8 + n8) * 128], in_=tp[:, 0:n8, :]), reads=[tp.r], writes=[kTg.r])
                        P.dma("sp", kTg.d, [(kTg[:, PAST:PAST + 64], D["kT"][g, :, sr:sr + 64])], reads=[RD["kT"]], writes=[kTg.r])
                        P.dma("sp", cst.d, [(cst[:], D["c2v"][b, :, g * 128:(g + 1) * 128].rearrange("(t p) d -> p t d", p=128))], reads=[cbf.r], writes=[cst.r])
                        P.op("pool", lambda e: e.tensor_copy(out=vag[:, 0:NPT, 0:128], in_=cst[:]), reads=[cst.r], writes=[vag.r])
                        P.dma("sp", vag.d, [(vag[0:64, NPT, 0:128], D["vS"][sr:sr + 64, g * 128:(g + 1) * 128])], reads=[RD["vS"]], writes=[vag.r])
                        P.op("dve", lambda e: e.memset(vag[:, :, 128:129], 1.0), writes=[vag.r])
                        qt = qtr.next(); mk = mkr.next(); zt = ztr.next(); ogt = ogr.next()
                        P.dma("sp", qt.d, [(v3(qt, 64)[1], D["qT"][4 * g:4 * g + 4, :, sr:sr + 64].rearrange("i p c -> p i c"))], reads=[RD["qT"]], writes=[qt.r])
                        P.dma("sp", mk.d, [(mk[:, 0:NPT + 1, 0:64], D["mkS"][b])], reads=[RD["mkS"]], writes=[mk.r])
                        P.dma("sp", zt.d, [(zt[0:64, :], D["zS"][sr:sr + 64, g * 512:(g + 1) * 512])], reads=[RD["zS"]], writes=[zt.r])
                        att_tile(g, qt, 64, NPT + 1, 64, mk, zt, ogt)
                        P.dma("sp", ogt.d, [(D["ogS"][sr:sr + 64, g * 512:(g + 1) * 512], ogt[0:64, :])], reads=[ogt.r], writes=[RD["ogS"]])
                P.barrier()

        nl = cfg.nlayers
        import os
        KSTOP = int(os.environ.get("KSTOP", "99"))
        for l in range(nl):
            kind = LAYER_KIND[l]
            KB = int(os.environ.get("KB", "99"))
            inproj(l)
            if KSTOP == 0 or (l == 1 and KB == 0):
                break
            if kind == "A":
                attn_diff(l)
            elif kind == "B":
                attn_swa(l)
            else:
                attn_dsa(l)
            if KSTOP == 1 or (l == 1 and KB == 1):
                break
            outproj(l, last=(l == nl - 1))
        P.barrier()
        print("program built: ninst", P.ninst, "nwait", P.nwait, "nsem", P.nsem)
    return nc


def rope_tables(cfg):
    UT, NT, PAST = cfg.UT, cfg.NT, cfg.PAST
    pos = np.zeros((UT,), np.float32)
    u = np.arange(cfg.U)
    pos[:cfg.U] = np.maximum(u - 112, 0)
    pos[cfg.U:cfg.U + 64] = PAST + np.arange(64)
    pos[cfg.U + 64:] = PAST + np.arange(64)
    out = []
    for rd in (32, 16):
        half = rd // 2
        inv = (np.float32(500000.0) ** (-(np.arange(half, dtype=np.float32) * np.float32(2.0) / np.float32(rd)))).astype(np.float32)
        ang = pos[:, None].astype(np.float32) * inv[None, :]
        out.append(np.concatenate([np.cos(ang), np.sin(ang)], axis=1).astype(np.float32))
    return out


def const_masks():
    r = np.arange(128)[:, None]
    c = np.arange(128)[None, :]
    dm = ((r < 64) | (c >= 64)).astype(np.float32)
    sw0 = (~((r < 64) & (c >= 64))).astype(np.float32)
    sw1 = dm
    bf = ml_dtypes.bfloat16
    return dm.astype(bf), np.stack([sw0, sw1]).astype(bf)


_CACHE = {}


def kernel(**inp):
    SEQ = inp["x_prompt"].shape[1]
    PAST = inp["cache_l0_k"].shape[1]
    NB = inp["x_prompt"].shape[0]
    cfg = Cfg(SEQ, PAST, nlayers=inp.pop("_nlayers", 4))
    key = (SEQ, PAST, cfg.nlayers)
    if key not in _CACHE:
        _CACHE[key] = build(cfg)
    nc = _CACHE[key]
    f32 = np.float32
    cs128, cs64 = rope_tables(cfg)
    dm, swm = const_masks()
    ident = np.eye(128, dtype=np.float32).astype(ml_dtypes.bfloat16)
    c = lambda a: np.ascontiguousarray(np.asarray(a, dtype=f32))
    common = dict(meta=c(inp["meta_tokens"]), sinks=c(inp["l1_sinks"]), fnorm=c(inp["final_norm"]),
                  ident=ident, cs128=cs128, cs64=cs64, dmask=dm, swm=swm)
    for l in range(4):
        common[f"normv{l}"] = c(inp[f"l{l}_norm"]); common[f"win{l}"] = c(inp[f"l{l}_w_in"]); common[f"wout{l}"] = c(inp[f"l{l}_w_out"])
    for l in (0, 3):
        common[f"lam{l}"] = c(np.stack([c(inp[f"l{l}_lam_q1"]), c(inp[f"l{l}_lam_k1"]), c(inp[f"l{l}_lam_q2"]), c(inp[f"l{l}_lam_k2"])]).T)
        common[f"subln{l}"] = c(inp[f"l{l}_subln"])
    in_maps = []
    for core in range(8):
        b = core % NB
        sb = slice(2 * core, 2 * core + 2)
        m = dict(common)
        m["xp"] = c(inp["x_prompt"][b]); m["xs"] = c(inp["x_sample"][sb]).reshape(128, 2048)
        for l in (0, 3):
            m[f"c{l}k"] = c(inp[f"cache_l{l}_k"][sb]).reshape(2, PAST, 2048)
            m[f"c{l}v"] = c(inp[f"cache_l{l}_v"][sb]).reshape(2, PAST, 2048)
        m["c1k"] = c(inp["cache_l1_k"][sb]).reshape(2, 128, 256); m["c1v"] = c(inp["cache_l1_v"][sb]).reshape(2, 128, 256)
        m["c2k"] = c(inp["cache_l2_k"][sb]).reshape(2, PAST, 512); m["c2v"] = c(inp["cache_l2_v"][sb]).reshape(2, PAST, 512)
        m["c2i"] = c(inp["cache_l2_kidx"][sb]).reshape(2, PAST, 64)
        in_maps.append(m)
    res = run_bass_kernel_spmd(nc, in_maps, core_ids=list(range(8))).results
    Tn = cfg.T
    P_ = lambda name, shp: np.stack([res[b][name] for b in range(NB)]).reshape((NB,) + shp)
    S_ = lambda name, shp: np.concatenate([res[cidx][name].reshape((2,) + shp) for cidx in range(8)], axis=0)
    outs = [P_("yp", (SEQ, 2048)), S_("ys", (64, 2048)),
            P_("p0k", (Tn, 8, 256)), P_("p0v", (Tn, 8, 256)), S_("s0k", (64, 8, 256)), S_("s0v", (64, 8, 256)),
            P_("p1k", (128, 4, 64)), P_("p1v", (128, 4, 64)), S_("s1k", (128, 4, 64)), S_("s1v", (128, 4, 64)),
            P_("p2k", (Tn, 4, 128)), P_("p2v", (Tn, 4, 128)), P_("p2i", (Tn, 64)),
            S_("s2k", (64, 4, 128)), S_("s2v", (64, 4, 128)), S_("s2i", (64, 64)),
            P_("p3k", (Tn, 8, 256)), P_("p3v", (Tn, 8, 256)), S_("s3k", (64, 8, 256)), S_("s3v", (64, 8, 256))]
    return tuple(np.ascontiguousarray(o.astype(np.float32)) for o in outs)
```
